# Optimizing a Trainium2 kernel written in Bass

```python
import jax, jax.numpy as jnp
from jax import lax
import numpy as np

D_MODEL = 2048
BATCH = 8
SEQ = 2048
DEPTH = 4
DEC_BATCH = 8
DEC_SEQ = 32
PAST_LEN = 2048

CHUNK = 64
D_A = 1024
D_B = 1024
D_MIX = D_A + D_B
P_IN = 2 * D_A + 3 * D_B
WIDTH_A = 31
WIDTH_B = 3
WIDTH_FFN = 3
D_FF = 5632
EPS = 1e-6

kernel_name = "hybrid_conformer_shortconv_streaming_step"


def rms_norm(x, g):
    xf = x.astype(jnp.float32)
    y = xf * lax.rsqrt(jnp.mean(xf * xf, axis=-1, keepdims=True) + EPS)
    return (y * g.astype(jnp.float32)).astype(x.dtype)


def layer_norm(x, g, b):
    xf = x.astype(jnp.float32)
    mu = jnp.mean(xf, axis=-1, keepdims=True)
    var = jnp.mean(jnp.square(xf - mu), axis=-1, keepdims=True)
    y = (xf - mu) * lax.rsqrt(var + EPS)
    return (y * g.astype(jnp.float32) + b.astype(jnp.float32)).astype(x.dtype)


def causal_dwconv(hist, x, w):
    k = w.shape[0]
    xp = jnp.concatenate([hist.astype(x.dtype), x], axis=1)
    y = lax.conv_general_dilated(
        xp, w[:, None, :].astype(x.dtype), window_strides=(1,), padding='VALID',
        dimension_numbers=('NWC', 'WIO', 'NWC'), feature_group_count=x.shape[-1])
    return y, xp[:, xp.shape[1] - (k - 1):, :]


def trunk_layer(x, h_a, h_b, h_f, norm1_g, w_in, conv_a_w, conv_a_b, ln_a_g, ln_a_b,
                conv_b_w, w_out, norm2_g, w_up, conv_ffn_w, w_down):
    h = rms_norm(x, norm1_g)
    z = jnp.einsum('btd,dp->btp', h, w_in)
    a_val, a_gate, b_b, b_c, b_h = jnp.split(
        z, [D_A, 2 * D_A, 2 * D_A + D_B, 2 * D_A + 2 * D_B], axis=-1)
    a = a_val * jax.nn.sigmoid(a_gate)
    a, new_a = causal_dwconv(h_a, a, conv_a_w)
    a = jax.nn.silu(layer_norm(a + conv_a_b, ln_a_g, ln_a_b))
    u, new_b = causal_dwconv(h_b, b_c * b_h, conv_b_w)
    y_b = b_b * u
    x = x + jnp.einsum('btm,md->btd', jnp.concatenate([a, y_b], axis=-1), w_out)
    h = rms_norm(x, norm2_g)
    u = jnp.einsum('btd,df->btf', h, w_up)
    u, new_f = causal_dwconv(h_f, u, conv_ffn_w)
    g, v = jnp.split(u, 2, axis=-1)
    x = x + jnp.einsum('btf,fd->btd', jax.nn.silu(g) * v, w_down)
    return x, new_a, new_b, new_f


def setup_inputs(seed: int = 0) -> dict:
    key = jax.random.key(seed)
    ks = jax.random.split(key, 20)
    f32 = jnp.float32
    nrm = lambda k, shape, s: (jax.random.normal(k, shape, f32) * s).astype(f32)
    return {
        "x_prompt": nrm(ks[0], (BATCH, SEQ, D_MODEL), 1.0),
        "x_sample": nrm(ks[1], (DEC_BATCH, DEC_SEQ, D_MODEL), 1.0),
        "state_conv_a": nrm(ks[2], (DEPTH, DEC_BATCH, WIDTH_A - 1, D_A), 0.5),
        "state_conv_b": nrm(ks[3], (DEPTH, DEC_BATCH, WIDTH_B - 1, D_B), 0.5),
        "state_ffn": nrm(ks[4], (DEPTH, DEC_BATCH, WIDTH_FFN - 1, 2 * D_FF), 1.0),
        "norm1_g": 1.0 + nrm(ks[5], (DEPTH, D_MODEL), 0.01),
        "w_in": nrm(ks[6], (DEPTH, D_MODEL, P_IN), D_MODEL ** -0.5),
        "conv_a_w": nrm(ks[7], (DEPTH, WIDTH_A, D_A), WIDTH_A ** -0.5),
        "conv_a_b": nrm(ks[8], (DEPTH, D_A), 0.01),
        "ln_a_g": 1.0 + nrm(ks[9], (DEPTH, D_A), 0.01),
        "ln_a_b": nrm(ks[10], (DEPTH, D_A), 0.01),
        "conv_b_w": nrm(ks[11], (DEPTH, WIDTH_B, D_B), WIDTH_B ** -0.5),
        "w_out": nrm(ks[12], (DEPTH, D_MIX, D_MODEL), D_MIX ** -0.5),
        "norm2_g": 1.0 + nrm(ks[13], (DEPTH, D_MODEL), 0.01),
        "w_up": nrm(ks[14], (DEPTH, D_MODEL, 2 * D_FF), D_MODEL ** -0.5),
        "conv_ffn_w": nrm(ks[15], (DEPTH, WIDTH_FFN, 2 * D_FF), WIDTH_FFN ** -0.5),
        "w_down": nrm(ks[16], (DEPTH, D_FF, D_MODEL), D_FF ** -0.5),
        "final_g": 1.0 + nrm(ks[17], (D_MODEL,), 0.01),
    }


def reference(x_prompt, x_sample, state_conv_a, state_conv_b, state_ffn,
              norm1_g, w_in, conv_a_w, conv_a_b, ln_a_g, ln_a_b, conv_b_w, w_out,
              norm2_g, w_up, conv_ffn_w, w_down, final_g):
    bp = x_prompt.shape[0]
    dt = x_prompt.dtype
    xp, xs = x_prompt, x_sample
    pa, pb, pf, sa, sb, sf = [], [], [], [], [], []
    for l in range(DEPTH):
        lw = (norm1_g[l], w_in[l], conv_a_w[l], conv_a_b[l], ln_a_g[l], ln_a_b[l],
              conv_b_w[l], w_out[l], norm2_g[l], w_up[l], conv_ffn_w[l], w_down[l])
        xp, na, nb, nf = trunk_layer(
            xp,
            jnp.zeros((bp, WIDTH_A - 1, D_A), dt),
            jnp.zeros((bp, WIDTH_B - 1, D_B), dt),
            jnp.zeros((bp, WIDTH_FFN - 1, 2 * D_FF), dt),
            *lw)
        pa.append(na); pb.append(nb); pf.append(nf)
        xs, na, nb, nf = trunk_layer(xs, state_conv_a[l], state_conv_b[l], state_ffn[l], *lw)
        sa.append(na); sb.append(nb); sf.append(nf)
    y_prompt = rms_norm(xp, final_g)
    y_sample = rms_norm(xs, final_g)
    new_conv_a_prompt = jnp.stack(pa, axis=0)
    new_conv_b_prompt = jnp.stack(pb, axis=0)
    new_ffn_prompt = jnp.stack(pf, axis=0)
    new_conv_a_sample = jnp.stack(sa, axis=0)
    new_conv_b_sample = jnp.stack(sb, axis=0)
    new_ffn_sample = jnp.stack(sf, axis=0)
    return (y_prompt, y_sample, new_conv_a_prompt, new_conv_b_prompt, new_ffn_prompt,
            new_conv_a_sample, new_conv_b_sample, new_ffn_sample)
```

```python
import numpy as np
from contextlib import ExitStack
import concourse.bass as bass
import concourse.mybir as mybir
from concourse.bass_utils import run_bass_kernel_spmd

F32 = mybir.dt.float32
BF16 = mybir.dt.bfloat16
AF = mybir.ActivationFunctionType
ALU = mybir.AluOpType

L = 4
D = 2048
DC = 16
CA = 8
CF = 44
DFF = 5632
HA = 30
KA = 31
TMAX = 704
EPS = 1e-6
NW = 6
NPS = 4
NTMP = 6
NPT = 0
PBW = 640

TILES = [
    dict(T=704, segs=[dict(kind='p', tok0=0, n=704, col=0, first=True, last=False)]),
    dict(T=704, segs=[dict(kind='p', tok0=704, n=704, col=0, first=False, last=False)]),
    dict(T=704, segs=[dict(kind='p', tok0=1408, n=640, col=0, first=False, last=True),
                      dict(kind='s', tok0=0, n=32, col=670, first=False, last=True)],
         gaps=[(640, 670), (702, 704)]),
]

ENGS = ['pe', 'act', 'dve', 'pool', 'sp']


class Prog:
    def __init__(self):
        self.q = {e: [] for e in ENGS}
        self.cnt = {e: 0 for e in ENGS}
        self.dcnt = {}
        self.lastw = {}
        self.readers = {}
        self.kview = {}
        self.rview = {}
        self.rkeys = {}
        self.rfence = {}
        self.out_tokens = []

    @staticmethod
    def _compress(toks):
        m = {}
        for t in toks:
            k = (t[0], t[1])
            if m.get(k, 0) < t[2]:
                m[k] = t[2]
        return [(a, b, n) for (a, b), n in m.items()]

    def _touch(self, k, waits):
        rv = self.kview.get(k[0])
        if rv is None:
            return
        reg, view = rv
        if self.rview.get(reg) != view:
            fence = []
            for kk in self.rkeys.get(reg, ()):
                if kk in self.lastw:
                    fence.append(self.lastw.pop(kk))
                fence.extend(self.readers.pop(kk, []))
            self.rfence[reg] = self._compress(fence)
            self.rview[reg] = view
            self.rkeys[reg] = set()
        self.rkeys[reg].add(k)
        waits.extend(self.rfence.get(reg, []))

    def op(self, eng, emit, reads=(), writes=(), dma=None, is_output=False):
        raw = []
        other = []
        for k in reads:
            self._touch(k, other)
            w = self.lastw.get(k)
            if w is not None:
                raw.append(w)
        for k in writes:
            self._touch(k, other)
            w = self.lastw.get(k)
            if w is not None:
                other.append(w)
            other.extend(self.readers.get(k, []))
        waits = list(raw)
        for t in other:
            if t[0] == 'e' and t[1] == eng:
                continue
            waits.append(t)
        if eng == 'pe':
            waits = [t for t in waits if not (t[0] == 'e' and t[1] == 'pe')]
        waits = self._compress(waits)
        if dma is None:
            self.cnt[eng] += 1
            tok = ('e', eng, self.cnt[eng])
        else:
            self.dcnt[dma] = self.dcnt.get(dma, 0) + 1
            tok = ('d', dma, self.dcnt[dma])
        for k in reads:
            self.readers.setdefault(k, []).append(tok)
        for k in writes:
            self.lastw[k] = tok
            self.readers[k] = []
        self.q[eng].append((waits, emit, tok))
        if is_output:
            self.out_tokens.append(tok)
        return tok

    def replay(self, eng, engine, esem, dsem):
        seen = {}

        def do_wait(t):
            if t[0] == 'e':
                sem, val = esem[t[1]], t[2]
            else:
                sem, val = dsem[t[1]], 16 * t[2]
            key = (t[0], t[1])
            if seen.get(key, 0) < val:
                engine.wait_ge(sem, val)
                seen[key] = val

        for waits, emit, tok in self.q[eng]:
            for t in waits:
                do_wait(t)
            ins = emit(engine)
            if tok[0] == 'e':
                ins.then_inc(esem[eng], 1)
            else:
                ins.then_inc(dsem[tok[1]], 16)
        if eng == 'sp':
            for t in self._compress(self.out_tokens):
                do_wait(t)


def build_program(n_layers=L):
    nc = bass.Bass("TRN2", target_bir_lowering=False)

    def din(name, shape):
        return nc.dram_tensor(name, shape, F32, kind="ExternalInput").ap()

    def dout(name, shape):
        return nc.dram_tensor(name, shape, F32, kind="ExternalOutput").ap()

    xp = din("xp", [2048, D]); xs = din("xs", [32, D])
    sa = din("sa", [L, 30, 1024]); sbd = din("sb", [L, 2, 1024]); sfd = din("sf", [L, 2, 2 * DFF])
    n1g = din("n1g", [L, D]); w_in = din("w_in", [L, D, 5120]); caw = din("caw", [L, KA, 1024])
    cab = din("cab", [L, 1024]); lng = din("lng", [L, 1024]); lnb = din("lnb", [L, 1024])
    cbw = din("cbw", [L, 3, 1024]); w_out = din("w_out", [L, D, D]); n2g = din("n2g", [L, D])
    w_up = din("w_up", [L, D, 2 * DFF]); cfw = din("cfw", [L, 3, 2 * DFF]); w_down = din("w_down", [L, DFF, D])
    fg = din("fg", [D])
    yp = dout("yp", [2048, D]); ys = dout("ys", [32, D])
    nap = dout("nap", [L, 30, 1024]); nbp = dout("nbp", [L, 2, 1024]); nfp = dout("nfp", [L, 2, 2 * DFF])
    nas = dout("nas", [L, 30, 1024]); nbs = dout("nbs", [L, 2, 1024]); nfs = dout("nfs", [L, 2, 2 * DFF])

    es = ExitStack()

    def sb(name, shape, dt=F32):
        return es.enter_context(nc.sbuf_tensor(name, shape, dt))

    X = sb("X", [128, DC, TMAX])
    H = sb("H", [128, DC, TMAX], BF16)
    PW = 8448 + NTMP * TMAX + 2 * 734 + 2 * 706
    PR = sb("PR", [128, PW])
    SQ = sb("SQ", [128, 3, TMAX], BF16)
    SQRT = sb("SQRT", [128, TMAX]); RSTD = sb("RSTD", [128, TMAX])
    MU = sb("MU", [128, TMAX]); VAR = sb("VAR", [128, TMAX])
    W = sb("W", [128, NW, 16, 128], BF16)
    PB = sb("PB", [128, L, PBW]); PBF = sb("PBF", [128, 16]); PSTF = sb("PSTF", [16, 128])
    HISTA = sb("HISTA", [128, L, CA, HA]); HISTB = sb("HISTB", [128, L, CA, 2])
    HISTF = sb("HISTF", [128, L, 2, CF, 2])
    SAT = sb("SAT", [128, L, 2, HA, 4]); SBT = sb("SBT", [128, L, 2, CA]); SFT = sb("SFT", [128, L, 2, 2 * CF])
    SA_in = sb("SA_in", [120, 256]); SB_in = sb("SB_in", [16, 128]); SF_in = sb("SF_in", [88, 2, 128])
    STA = sb("STA", [128, 2, 2, HA, 4]); STB = sb("STB", [128, 2, 2, CA]); STF = sb("STF", [128, 2, 2, 2 * CF])
    OUTA = sb("OUTA", [120, 2, 256]); OUTB = sb("OUTB", [16, 2, 128]); OUTF = sb("OUTF", [88, 2, 2, 128])
    IDENT = sb("IDENT", [128, 128]); ONESB = sb("ONESB", [128, 128], BF16); ONESB2 = sb("ONESB2", [128, 128], BF16)
    PST = sb("PST", [128, 5, 128])
    ZERO = sb("ZERO", [128, 32]); EPSC = sb("EPSC", [128, 1]); DUMMY = sb("DUMMY", [128, 4])
    PS = es.enter_context(nc.psum_tensor("PS", [128, 2 * NPS, 512], F32))

    def emit_all(P, slabs, dry):
        for k in ("R", "MB", "TMP", "AGLU", "AGLUH", "CH", "CHH"):
            P.kview[k] = ("PR", "mix")
        for k in ("ACTQ", "UGV", "UGVH", "TMPF"):
            P.kview[k] = ("PR", "ffn")
        P.kview["XIN"] = ("PR", "io_in")
        P.kview["YOUT"] = ("PR", "io_out")

        R_bf = PR[:, 0:5632].bitcast(BF16)
        MB_bf = PR[:, 5632:8448].bitcast(BF16)
        ACTQ_bf = PR[:, 0:7744].bitcast(BF16)

        def ACONV(c, Tt): return PR[:, c * TMAX: c * TMAX + Tt]
        def MAc(c, a, b): return R_bf[:, c * TMAX + a: c * TMAX + b]
        def MBc(c, a, b): return MB_bf[:, c * TMAX + a: c * TMAX + b]
        def TMPb(i, Tt): return PR[:, 8448 + i * TMAX: 8448 + i * TMAX + Tt]
        def TMPbf(i, a, b): return PR[:, 8448 + i * TMAX: 8448 + (i + 1) * TMAX].bitcast(BF16)[:, a:b]
        AG0 = 8448 + NTMP * TMAX
        def AGLUb(i, a, b): return PR[:, AG0 + i * 734 + a: AG0 + i * 734 + b]
        CH0 = AG0 + 2 * 734
        def CHb(i, a, b): return PR[:, CH0 + i * 706 + a: CH0 + i * 706 + b]
        def ACTQc(bq, i, a, b): return ACTQ_bf[:, (bq * 11 + i) * TMAX + a: (bq * 11 + i) * TMAX + b]
        def UGVb(i): return PR[:, 7744 + i * 1412: 7744 + (i + 1) * 1412].rearrange("p (g t) -> p g t", g=2)
        def TMPFb(i, Tt): return PR[:, 10568 + i * TMAX: 10568 + i * TMAX + Tt]
        def XINb(i): return PR[:, i * 2048:(i + 1) * 2048]
        def YOUTb(i): return PR[:, i * 2048:(i + 1) * 2048]

        def v3(ap):
            return ap.rearrange("p (a n) -> p a n", a=2)

        ctr = dict(ps=0, w=0, tmp=0, sq=0, aglu=0, ch=0, ugv=0, tmpf=0, xin=0, yout=0, alt=0, alt2=0)

        ps_reserved = set()

        def nxt(name, n):
            while True:
                v = ctr[name]
                ctr[name] = (v + 1) % n
                if name != 'ps' or v not in ps_reserved:
                    return v

        def psv(s, Th):
            return PS[:, 2 * s:2 * s + 2, 0:Th]

        P.op('pool', lambda g: g.memset(IDENT[:], 0.0), writes=[("IDENT",)])
        P.op('pool', lambda g: g.affine_select(out=IDENT[:], in_=IDENT[:], pattern=[[-1, 128]],
                                               compare_op=ALU.not_equal, fill=1.0, base=0, channel_multiplier=1),
             reads=[("IDENT",)], writes=[("IDENT",)])
        P.op('dve', lambda v: v.memset(ONESB[:], 1.0 / D), writes=[("ONESB",)])
        P.op('dve', lambda v: v.memset(ONESB2[:], 1.0 / 1024), writes=[("ONESB2",)])
        P.op('dve', lambda v: v.memset(ZERO[:], 0.0), writes=[("ZERO",)])
        P.op('dve', lambda v: v.memset(EPSC[:], EPS), writes=[("EPSC",)])

        def copy_alt(out, in_, reads, writes):
            if nxt('alt', 2) == 0:
                P.op('act', lambda a: a.activation(out=out, in_=in_, func=AF.Copy), reads=reads, writes=writes)
            else:
                P.op('dve', lambda v: v.tensor_copy(out=out, in_=in_), reads=reads, writes=writes)

        def act_copy(out, in_, reads, writes):
            P.op('act', lambda a: a.activation(out=out, in_=in_, func=AF.Copy), reads=reads, writes=writes)

        def load_params():
            def rows2(ap1d_or_2d, pat, **kw):
                return ap1d_or_2d.rearrange(pat, **kw)

            for l in range(n_layers):
                pieces = [
                    (rows2(n1g[l], "(c n) -> c n", n=128), 0, 16),
                    (rows2(n2g[l], "(c n) -> c n", n=128), 16, 16),
                    (rows2(cab[l], "(c n) -> c n", n=128), 32, 8),
                    (rows2(lng[l], "(c n) -> c n", n=128), 40, 8),
                    (rows2(lnb[l], "(c n) -> c n", n=128), 48, 8),
                    (rows2(cbw[l], "k (c n) -> (k c) n", n=128), 56, 24),
                    (rows2(caw[l], "k (c n) -> (k c) n", n=128), 80, 248),
                    (rows2(cfw[l], "k (c n) -> (k c) n", n=128), 328, 264),
                ]
                for src, r0, nr in pieces:
                    a = 0
                    while a < nr:
                        g = (r0 + a) // 128
                        p0 = (r0 + a) % 128
                        m = min(nr - a, 128 - p0)
                        P.op('act' if nxt('alt2', 2) == 0 else 'sp', lambda s, src=src, a=a, m=m, g=g, p0=p0:
                             s.dma_start(out=PST[p0:p0 + m, g, :], in_=src[a:a + m, :]),
                             writes=[("PST", g)], dma="par")
                        a += m
                s_ = nxt('ps', NPS)

                def emit_pt(t, s_=s_):
                    for g in range(5):
                        rows = min(128, 592 - g * 128)
                        ins = t.transpose(PS[:, 2 * s_ + g // 4, (g % 4) * 128:(g % 4) * 128 + rows],
                                          PST[0:rows, g, :], IDENT[0:rows, 0:rows])
                    return ins
                P.op('pe', emit_pt, reads=[("PST", g) for g in range(5)] + [("IDENT",)], writes=[("ps", s_)])
                P.op('dve', lambda v, s_=s_, l=l: v.tensor_copy(out=PB[:, l, 0:512], in_=PS[:, 2 * s_, 0:512]),
                     reads=[("ps", s_)], writes=[("PB",)])
                P.op('dve', lambda v, s_=s_, l=l: v.tensor_copy(out=PB[:, l, 512:592], in_=PS[:, 2 * s_ + 1, 0:80]),
                     reads=[("ps", s_)], writes=[("PB",)])
            P.op('sp', lambda s: s.dma_start(out=PSTF[:, :], in_=fg.rearrange("(c n) -> c n", n=128)),
                 writes=[("PSTF",)], dma="parf")
            s_ = nxt('ps', NPS)
            P.op('pe', lambda t, s_=s_: t.transpose(PS[:, 2 * s_, 0:16], PSTF[0:16, :], IDENT[0:16, 0:16]),
                 reads=[("PSTF",), ("IDENT",)], writes=[("ps", s_)])
            P.op('dve', lambda v, s_=s_: v.tensor_copy(out=PBF[:, :], in_=PS[:, 2 * s_, 0:16]),
                 reads=[("ps", s_)], writes=[("PBF",)])


        def load_states():
            for l in range(n_layers):
                P.op('sp', lambda s, l=l: s.dma_start(out=SA_in[:, :], in_=sa[l].rearrange("t (q n) -> (t q) n", n=256)),
                     writes=[("SA_in",)], dma="sa_in")
                s_ = nxt('ps', NPS)

                def emit_sa(t, s_=s_):
                    for j in range(2):
                        ins = t.transpose(PS[:, 2 * s_, j * 128:j * 128 + 120], SA_in[0:120, j * 128:(j + 1) * 128],
                                          IDENT[0:120, 0:120])
                    return ins
                P.op('pe', emit_sa, reads=[("SA_in",), ("IDENT",)], writes=[("ps", s_)])
                P.op('act', lambda a, s_=s_, l=l: a.activation(
                    out=SAT[:, l].rearrange("p j t q -> p j (t q)"),
                    in_=PS[:, 2 * s_, 0:256].rearrange("p (j n) -> p j n", j=2)[:, :, 0:120], func=AF.Copy),
                    reads=[("ps", s_)], writes=[("SAT", l)])
                P.op('sp', lambda s, l=l: s.dma_start(out=SB_in[:, :], in_=sbd[l].rearrange("t (c n) -> (t c) n", n=128)),
                     writes=[("SB_in",)], dma="sb_in")
                s_ = nxt('ps', NPS)
                P.op('pe', lambda t, s_=s_: t.transpose(PS[:, 2 * s_, 0:16], SB_in[0:16, :], IDENT[0:16, 0:16]),
                     reads=[("SB_in",), ("IDENT",)], writes=[("ps", s_)])
                P.op('act', lambda a, s_=s_, l=l: a.activation(
                    out=SBT[:, l].rearrange("p t c -> p (t c)"), in_=PS[:, 2 * s_, 0:16], func=AF.Copy),
                    reads=[("ps", s_)], writes=[("SBT", l)])
                P.op('sp', lambda s, l=l: s.dma_start(out=SF_in[:, :, :], in_=sfd[l].rearrange("t (c n) -> c t n", n=128)),
                     writes=[("SF_in",)], dma="sf_in")
                s_ = nxt('ps', NPS)

                def emit_sf(t, s_=s_):
                    for tt in range(2):
                        ins = t.transpose(PS[:, 2 * s_, tt * 128:tt * 128 + 88], SF_in[0:88, tt, :], IDENT[0:88, 0:88])
                    return ins
                P.op('pe', emit_sf, reads=[("SF_in",), ("IDENT",)], writes=[("ps", s_)])
                P.op('act', lambda a, s_=s_, l=l: a.activation(
                    out=SFT[:, l], in_=PS[:, 2 * s_, 0:256].rearrange("p (t n) -> p t n", t=2)[:, :, 0:88], func=AF.Copy),
                    reads=[("ps", s_)], writes=[("SFT", l)])


        wst = dict(n=0, issued=0)

        def wload(src, nk):
            n = wst['n']
            wst['n'] += 1
            if dry:
                slabs.append((src, nk))
                return n % NW
            upto = min(n + NW - 1, len(slabs) - 1)
            while wst['issued'] <= upto:
                m = wst['issued']
                s_src, s_nk = slabs[m]
                i = m % NW
                P.op('pool', lambda g, i=i, s_src=s_src, s_nk=s_nk: g.dma_start(out=W[:, i, 0:s_nk, :], in_=s_src),
                     writes=[("W", i)], dma="W%d" % i)
                wst['issued'] += 1
            return n % NW

        def mm_job(nk, lhsT_fn, rhs_fn, reads, Th, fine=None):
            s = nxt('ps', NPS)
            if fine is not None:
                for k in range(nk):
                    def emit_k(t, k=k):
                        lt = lhsT_fn(k)
                        for n in range(2):
                            ins = t.matmul(PS[:, 2 * s + n, 0:Th], lhsT=lt, rhs=rhs_fn(k, n),
                                           start=(k == 0), stop=(k == nk - 1))
                        return ins
                    P.op('pe', emit_k, reads=[fine(k)] + reads, writes=[("ps", s)])
                return s

            def emit(t):
                ins = None
                for k in range(nk):
                    lt = lhsT_fn(k)
                    for n in range(2):
                        ins = t.matmul(PS[:, 2 * s + n, 0:Th], lhsT=lt, rhs=rhs_fn(k, n),
                                       start=(k == 0), stop=(k == nk - 1))
                return ins
            P.op('pe', emit, reads=reads, writes=[("ps", s)])
            return s

        XK = [("X", c) for c in range(DC)]
        HK = [("H", c) for c in range(DC)]

        def act_preload(func, col):
            P.op('act', lambda a: a.activation(out=DUMMY[:, col:col + 1], in_=EPSC[:, 0:1], func=func),
                 reads=[("EPSC",)], writes=[("DUMMY", col)])

        def rms_begin():
            s = nxt('ps', NPS)
            ps_reserved.add(s)
            return dict(s=s)

        def rms_chunk(st, c, Tt, Th):
            s = st['s']
            i = nxt('sq', 3)
            P.op('act', lambda a, i=i, c=c: a.activation(out=SQ[:, i, 0:Tt], in_=X[:, c, 0:Tt], func=AF.Square),
                 reads=[("X", c)], writes=[("SQ", i)])

            def emit(t, i=i, c=c):
                for n in range(2):
                    ins = t.matmul(PS[:, 2 * s + n, 0:Th], lhsT=ONESB[:, :], rhs=SQ[:, i, n * Th:(n + 1) * Th],
                                   start=(c == 0), stop=(c == DC - 1))
                return ins
            P.op('pe', emit, reads=[("SQ", i), ("ONESB",)], writes=[("ps", s)])

        def rms_end(st, Tt, Th, next_func=None):
            s = st['s']
            act_preload(AF.Sqrt, 0)
            P.op('act', lambda a: a.activation(out=v3(SQRT[:, 0:Tt]), in_=psv(s, Th), func=AF.Sqrt, bias=EPSC[:, 0:1]),
                 reads=[("ps", s), ("EPSC",)], writes=[("SQRT",)])
            ps_reserved.discard(s)
            if next_func is not None:
                act_preload(next_func, 1)
            P.op('dve', lambda v: v.reciprocal(out=RSTD[:, 0:Tt], in_=SQRT[:, 0:Tt]),
                 reads=[("SQRT",)], writes=[("RSTD",)])

        def rms_stats(Tt, Th, next_func=None):
            st = rms_begin()
            for c in range(DC):
                rms_chunk(st, c, Tt, Th)
            rms_end(st, Tt, Th, next_func)

        def rms_apply(Tt, gcol, out_fn, out_key):
            for c in range(DC):
                P.op('dve', lambda v, c=c: v.scalar_tensor_tensor(
                    out=out_fn(c), in0=X[:, c, 0:Tt], scalar=gcol(c), in1=RSTD[:, 0:Tt], op0=ALU.mult, op1=ALU.mult),
                    reads=[("X", c), ("RSTD",), ("PB",), ("PBF",)], writes=[out_key(c)])

        def do_tile(ti, tile):
            Tt = tile['T']
            Th = Tt // 2
            segs = tile['segs']

            for g0, g1 in tile.get('gaps', []):
                P.op('dve', lambda v, g0=g0, g1=g1: v.memset(X[:, :, g0:g1], 0.0), writes=XK)
            for seg in segs:
                src = xp if seg['kind'] == 'p' else xs
                a = 0
                while a < seg['n']:
                    nb = min(128, seg['n'] - a)
                    tok = seg['tok0'] + a
                    col = seg['col'] + a
                    xi = nxt('xin', 2)
                    P.op('sp', lambda s, xi=xi, nb=nb, tok=tok, src=src:
                         s.dma_start(out=XINb(xi)[0:nb, :], in_=src[tok:tok + nb, :]),
                         writes=[("XIN", xi)], dma="XIN%d" % xi)
                    for half in range(2):
                        s_ = nxt('ps', NPS)

                        def emit(t, s_=s_, xi=xi, nb=nb, half=half):
                            for j in range(8):
                                cc = half * 8 + j
                                ins = t.transpose(PS[:, 2 * s_ + j // 4, (j % 4) * 128:(j % 4) * 128 + nb],
                                                  XINb(xi)[0:nb, cc * 128:(cc + 1) * 128], IDENT[0:nb, 0:nb])
                            return ins
                        P.op('pe', emit, reads=[("XIN", xi), ("IDENT",)], writes=[("ps", s_)])
                        copy_alt(X[:, half * 8:half * 8 + 8, col:col + nb],
                                 PS[:, 2 * s_:2 * s_ + 2, :].rearrange("p a (b n) -> p (a b) n", n=128)[:, :, 0:nb],
                                 reads=[("ps", s_)], writes=[("X", half * 8 + j) for j in range(8)])
                    a += nb

            if ti == 0:
                load_params()
            nready = dict(v=False)

            def do_layer(l):
                def pb(col):
                    return PB[:, l, col:col + 1]

                if not nready['v']:
                    rms_stats(Tt, Th, AF.Sigmoid)
                nready['v'] = False
                rms_apply(Tt, lambda c: pb(c), lambda c: H[:, c, 0:Tt], lambda c: ("H", c))

                w_in_v = w_in[l].rearrange("(kc p) n -> p kc n", p=128)

                def inproj(group, c):
                    col0 = group * 1024 + c * 128
                    wi = wload(w_in_v[:, :, col0:col0 + 128], 16)
                    if False and c == 0 and group in (0, 1):
                        return mm_job(16, lambda k, wi=wi: W[:, wi, k, :],
                                      lambda k, n: H[:, k, n * Th:(n + 1) * Th], [("W", wi)], Th,
                                      fine=lambda k: ("H", k))
                    return mm_job(16, lambda k, wi=wi: W[:, wi, k, :],
                                  lambda k, n: H[:, k, n * Th:(n + 1) * Th], HK + [("W", wi)], Th)

                def hist_ops(seg_list, H_, bufslice, hkey, dkey, zero_src, hist_src, state_src, hist_dst, st_dst, l=l):
                    for si, seg in enumerate(seg_list):
                        col, n = seg['col'], seg['n']
                        if seg['kind'] == 'p' and seg['first']:
                            srcap, rk = zero_src, [("ZERO",)]
                        elif seg['kind'] == 'p':
                            srcap, rk = hist_src, [("HIST",)]
                        else:
                            srcap, rk = state_src, [("SAT", l), ("SBT", l), ("SFT", l)]
                        act_copy(bufslice(col, col + H_), srcap, reads=rk + ([dkey] if col > 0 else []), writes=[hkey])
                        tail = bufslice(col + n, col + n + H_)
                        if seg['kind'] == 'p' and not seg['last']:
                            act_copy(hist_dst, tail, reads=[dkey], writes=[("HIST",)])
                        if seg['last']:
                            act_copy(st_dst(si), tail, reads=[dkey], writes=[("ST", si)])

                last_tile = segs[0]['last']

                acc1_slot = [None]

                def doA(c, pre_cb=None, after_cb=None, mid_cb=None):
                    if pre_cb is not None:
                        pre_cb()
                    sv = inproj(0, c)
                    sg = inproj(1, c)
                    ts_ = nxt('tmp', NTMP)
                    P.op('act', lambda a, ts_=ts_, sg=sg: a.activation(out=v3(TMPb(ts_, Tt)), in_=psv(sg, Th), func=AF.Sigmoid),
                         reads=[("ps", sg)], writes=[("TMP", ts_)])
                    ag = nxt('aglu', 2)
                    P.op('dve', lambda v, ag=ag, sv=sv, ts_=ts_: v.tensor_tensor(
                        out=v3(AGLUb(ag, HA, HA + Tt)), in0=psv(sv, Th), in1=v3(TMPb(ts_, Tt)), op=ALU.mult),
                        reads=[("ps", sv), ("TMP", ts_)], writes=[("AGLU", ag)])
                    j_, q_ = c % 2, c // 2
                    hist_ops(segs, HA, lambda a, b, ag=ag: AGLUb(ag, a, b), ("AGLUH", ag), ("AGLU", ag),
                             ZERO[:, 0:HA], HISTA[:, l, c, :], SAT[:, l, j_, :, q_], HISTA[:, l, c, :],
                             lambda si, j_=j_, q_=q_: STA[:, si, j_, :, q_])
                    acc0 = ACONV(c, Tt)
                    t1 = nxt('tmp', NTMP)
                    acc1_slot[0] = t1
                    acc1 = TMPb(t1, Tt)
                    k0 = [("R", 2 * c), ("R", 2 * c + 1)]
                    k1 = [("TMP", t1)]
                    src_keys = [("AGLU", ag), ("AGLUH", ag), ("PB",)]
                    P.op('act', lambda a, ag=ag, acc0=acc0, c=c: a.activation(
                        out=acc0, in_=AGLUb(ag, 0, Tt), func=AF.Identity, scale=pb(80 + c), bias=pb(32 + c)),
                        reads=src_keys, writes=k0)
                    P.op('act', lambda a, ag=ag, acc1=acc1, c=c: a.activation(
                        out=acc1, in_=AGLUb(ag, 1, 1 + Tt), func=AF.Copy, scale=pb(80 + 8 + c)),
                        reads=src_keys, writes=k1)
                    if after_cb is not None:
                        after_cb()
                    if NPT > 0:
                        a2 = nxt('tmp', NTMP)
                        t2 = nxt('tmp', NTMP)
                        acc2 = TMPb(a2, Tt)
                        tmp2 = TMPb(t2, Tt)
                        for k in range(KA - NPT, KA):
                            if k == KA - NPT:
                                P.op('pool', lambda g, ag=ag, acc2=acc2, c=c, k=k: g.tensor_scalar(
                                    out=acc2, in0=AGLUb(ag, k, k + Tt), scalar1=pb(80 + k * 8 + c), scalar2=None, op0=ALU.mult),
                                    reads=src_keys, writes=[("TMP", a2)])
                            else:
                                P.op('pool', lambda g, ag=ag, tmp2=tmp2, c=c, k=k: g.tensor_scalar(
                                    out=tmp2, in0=AGLUb(ag, k, k + Tt), scalar1=pb(80 + k * 8 + c), scalar2=None, op0=ALU.mult),
                                    reads=src_keys, writes=[("TMP", t2)])
                                P.op('pool', lambda g, acc2=acc2, tmp2=tmp2: g.tensor_tensor(
                                    out=acc2, in0=acc2, in1=tmp2, op=ALU.add),
                                    reads=[("TMP", a2), ("TMP", t2)], writes=[("TMP", a2)])
                    for k in range(2, KA - NPT):
                        if mid_cb is not None and k == 16:
                            mid_cb()
                        acc, kk = (acc0, k0) if k % 2 == 0 else (acc1, k1)
                        P.op('dve', lambda v, ag=ag, acc=acc, c=c, k=k: v.scalar_tensor_tensor(
                            out=acc, in0=AGLUb(ag, k, k + Tt), scalar=pb(80 + k * 8 + c), in1=acc, op0=ALU.mult, op1=ALU.add),
                            reads=src_keys + kk, writes=kk)
                    P.op('dve', lambda v, acc0=acc0, acc1=acc1: v.tensor_tensor(out=acc0, in0=acc0, in1=acc1, op=ALU.add),
                         reads=k0 + k1, writes=k0)
                    if NPT > 0:
                        P.op('dve', lambda v, acc0=acc0, acc2=acc2: v.tensor_tensor(out=acc0, in0=acc0, in1=acc2, op=ALU.add),
                             reads=k0 + [("TMP", a2)], writes=k0)

                def doB1(c, st):
                    sh = inproj(4, c)
                    sc = inproj(3, c)
                    bh = nxt('tmp', NTMP)
                    P.op('act', lambda a, bh=bh, sh=sh: a.activation(out=v3(TMPb(bh, Tt)), in_=psv(sh, Th), func=AF.Copy),
                         reads=[("ps", sh)], writes=[("TMP", bh)])
                    st.update(sc=sc, bh=bh)

                def doB2(c, st):
                    sc, bh = st['sc'], st['bh']
                    ch = nxt('ch', 2)
                    P.op('dve', lambda v, ch=ch, sc=sc, bh=bh: v.tensor_tensor(
                        out=v3(CHb(ch, 2, 2 + Tt)), in0=psv(sc, Th), in1=v3(TMPb(bh, Tt)), op=ALU.mult),
                        reads=[("ps", sc), ("TMP", bh)], writes=[("CH", ch)])
                    hist_ops(segs, 2, lambda a, b, ch=ch: CHb(ch, a, b), ("CHH", ch), ("CH", ch),
                             ZERO[:, 0:2], HISTB[:, l, c, :], SBT[:, l, :, c], HISTB[:, l, c, :],
                             lambda si, c=c: STB[:, si, :, c])
                    u3 = nxt('tmp', NTMP)
                    srck = [("CH", ch), ("CHH", ch), ("PB",)]
                    P.op('act', lambda a, u3=u3, ch=ch, c=c: a.activation(
                        out=TMPb(u3, Tt), in_=CHb(ch, 0, Tt), func=AF.Copy, scale=pb(56 + c)),
                        reads=srck, writes=[("TMP", u3)])
                    st.update(ch=ch, u3=u3, srck=srck)

                def doB3(c, st):
                    ch, u3, srck = st['ch'], st['u3'], st['srck']
                    for k in (1, 2):
                        P.op('dve', lambda v, u3=u3, ch=ch, c=c, k=k: v.scalar_tensor_tensor(
                            out=TMPb(u3, Tt), in0=CHb(ch, k, k + Tt), scalar=pb(56 + k * 8 + c), in1=TMPb(u3, Tt),
                            op0=ALU.mult, op1=ALU.add), reads=srck + [("TMP", u3)], writes=[("TMP", u3)])
                    sbb = inproj(2, c)
                    P.op('dve', lambda v, sbb=sbb, u3=u3, c=c: v.tensor_tensor(
                        out=v3(MBc(c, 0, Tt)), in0=psv(sbb, Th), in1=v3(TMPb(u3, Tt)), op=ALU.mult),
                        reads=[("ps", sbb), ("TMP", u3)], writes=[("MB", c)])

                for c in range(CA):
                    st = {}
                    doA(c, pre_cb=lambda c=c, st=st: doB1(c, st), after_cb=lambda c=c, st=st: doB2(c, st),
                        mid_cb=lambda c=c, st=st: doB3(c, st))

                s_mu = nxt('ps', NPS)
                s_e2 = nxt('ps', NPS)
                for c in range(CA):
                    sq = nxt('tmp', NTMP)
                    if c < CA - 1 and sq == acc1_slot[0]:
                        sq = nxt('tmp', NTMP)
                    kc = [("R", 2 * c), ("R", 2 * c + 1)]
                    P.op('act', lambda a, sq=sq, c=c: a.activation(out=TMPbf(sq, 0, Tt), in_=ACONV(c, Tt), func=AF.Square),
                         reads=kc, writes=[("TMP", sq)])
                    P.op('act', lambda a, sq=sq, c=c: a.activation(out=TMPbf(sq, TMAX, TMAX + Tt), in_=ACONV(c, Tt), func=AF.Copy),
                         reads=kc, writes=[("TMP", sq)])

                    def emit(t, c=c, sq=sq):
                        for n in range(2):
                            t.matmul(PS[:, 2 * s_mu + n, 0:Th], lhsT=ONESB2[:, :], rhs=TMPbf(sq, TMAX + n * Th, TMAX + (n + 1) * Th),
                                     start=(c == 0), stop=(c == CA - 1))
                        for n in range(2):
                            ins = t.matmul(PS[:, 2 * s_e2 + n, 0:Th], lhsT=ONESB2[:, :], rhs=TMPbf(sq, n * Th, (n + 1) * Th),
                                           start=(c == 0), stop=(c == CA - 1))
                        return ins
                    P.op('pe', emit, reads=[("TMP", sq), ("ONESB2",)], writes=[("ps", s_mu), ("ps", s_e2)])
                act_preload(AF.Sqrt, 0)
                P.op('act', lambda a: a.activation(out=v3(MU[:, 0:Tt]), in_=psv(s_mu, Th), func=AF.Copy),
                     reads=[("ps", s_mu)], writes=[("MU",)])
                P.op('dve', lambda v: v.tensor_tensor(out=VAR[:, 0:Tt], in0=MU[:, 0:Tt], in1=MU[:, 0:Tt], op=ALU.mult),
                     reads=[("MU",)], writes=[("VAR",)])
                P.op('dve', lambda v: v.tensor_tensor(out=v3(VAR[:, 0:Tt]), in0=psv(s_e2, Th), in1=v3(VAR[:, 0:Tt]), op=ALU.subtract),
                     reads=[("ps", s_e2), ("VAR",)], writes=[("VAR",)])
                P.op('act', lambda a: a.activation(out=SQRT[:, 0:Tt], in_=VAR[:, 0:Tt], func=AF.Sqrt, bias=EPSC[:, 0:1]),
                     reads=[("VAR",), ("EPSC",)], writes=[("SQRT",)])
                act_preload(AF.Silu, 1)
                P.op('dve', lambda v: v.reciprocal(out=RSTD[:, 0:Tt], in_=SQRT[:, 0:Tt]),
                     reads=[("SQRT",)], writes=[("RSTD",)])
                for c0 in range(0, CA, 2):
                    tt_ = [nxt('tmp', NTMP), nxt('tmp', NTMP)]
                    for j in range(2):
                        c = c0 + j
                        P.op('dve', lambda v, c=c, t_=tt_[j]: v.tensor_tensor(
                            out=TMPb(t_, Tt), in0=ACONV(c, Tt), in1=MU[:, 0:Tt], op=ALU.subtract),
                            reads=[("R", 2 * c), ("R", 2 * c + 1), ("MU",)], writes=[("TMP", tt_[j])])
                    for j in range(2):
                        P.op('dve', lambda v, t_=tt_[j]: v.tensor_tensor(
                            out=TMPb(t_, Tt), in0=TMPb(t_, Tt), in1=RSTD[:, 0:Tt], op=ALU.mult),
                            reads=[("TMP", tt_[j]), ("RSTD",)], writes=[("TMP", tt_[j])])
                    for j in range(2):
                        c = c0 + j
                        P.op('act', lambda a, c=c, t_=tt_[j]: a.activation(
                            out=MAc(c, 0, Tt), in_=TMPb(t_, Tt), func=AF.Silu, scale=pb(40 + c), bias=pb(48 + c)),
                            reads=[("TMP", tt_[j]), ("PB",)], writes=[("R", c)])

                if last_tile:
                    for si, seg in enumerate(segs):
                        na_dst = nap if seg['kind'] == 'p' else nas
                        nb_dst = nbp if seg['kind'] == 'p' else nbs
                        s_ = nxt('ps', NPS)

                        def emit(t, s_=s_, si=si):
                            for j in range(2):
                                ins = t.transpose(PS[0:120, 2 * s_, j * 128:(j + 1) * 128],
                                                  STA[:, si, j].rearrange("p t q -> p (t q)"), IDENT[:, :])
                            return ins
                        P.op('pe', emit, reads=[("ST", si), ("IDENT",)], writes=[("ps", s_)])
                        act_copy(OUTA[:, si, :], PS[0:120, 2 * s_, 0:256], reads=[("ps", s_)], writes=[("OUTA", si)])
                        P.op('sp', lambda s, si=si, na_dst=na_dst: s.dma_start(
                            out=na_dst[l].rearrange("t (q n) -> (t q) n", n=256), in_=OUTA[:, si, :]),
                            reads=[("OUTA", si)], dma="OA%d" % si, is_output=True)
                        s_ = nxt('ps', NPS)
                        P.op('pe', lambda t, s_=s_, si=si: t.transpose(
                            PS[0:16, 2 * s_, 0:128], STB[:, si].rearrange("p t c -> p (t c)"), IDENT[:, :]),
                            reads=[("ST", si), ("IDENT",)], writes=[("ps", s_)])
                        act_copy(OUTB[:, si, :], PS[0:16, 2 * s_, 0:128], reads=[("ps", s_)], writes=[("OUTB", si)])
                        P.op('sp', lambda s, si=si, nb_dst=nb_dst: s.dma_start(
                            out=nb_dst[l].rearrange("t (c n) -> (t c) n", n=128), in_=OUTB[:, si, :]),
                            reads=[("OUTB", si)], dma="OB%d" % si, is_output=True)

                w_out_v = w_out[l].rearrange("(kc p) n -> p kc n", p=128)
                MK = [("R", c) for c in range(CA)] + [("MB", c) for c in range(CA)]

                def m_rhs(k, n):
                    if k < CA:
                        return MAc(k, n * Th, (n + 1) * Th)
                    return MBc(k - CA, n * Th, (n + 1) * Th)
                st2 = rms_begin()
                pend = []
                for j in range(DC):
                    wi = wload(w_out_v[:, :, j * 128:(j + 1) * 128], 16)
                    s_ = mm_job(16, lambda k, wi=wi: W[:, wi, k, :], m_rhs, MK + [("W", wi)], Th)
                    P.op('dve', lambda v, s_=s_, j=j: v.tensor_tensor(
                        out=v3(X[:, j, 0:Tt]), in0=psv(s_, Th), in1=v3(X[:, j, 0:Tt]), op=ALU.add),
                        reads=[("ps", s_), ("X", j)], writes=[("X", j)])
                    pend.append(j)
                    if len(pend) > 2:
                        rms_chunk(st2, pend.pop(0), Tt, Th)
                    P.op('act', lambda a, j=j: a.activation(out=H[:, j, 0:Tt], in_=X[:, j, 0:Tt], func=AF.Copy, scale=pb(16 + j)),
                         reads=[("X", j), ("PB",)], writes=[("H", j)])
                for j in pend:
                    rms_chunk(st2, j, Tt, Th)

                rms_end(st2, Tt, Th, AF.Silu)
                w_up_v = w_up[l].rearrange("(kc p) n -> p kc n", p=128)

                def up_quarter(q):
                    bq = q % 2
                    for i in range(11):
                        f = q * 11 + i
                        ss = []
                        for gv in range(2):
                            col0 = gv * DFF + f * 128
                            wi = wload(w_up_v[:, :, col0:col0 + 128], 16)
                            if False and f == 0:
                                ss.append(mm_job(16, lambda k, wi=wi: W[:, wi, k, :],
                                                 lambda k, n: H[:, k, n * Th:(n + 1) * Th], [("W", wi)], Th,
                                                 fine=lambda k: ("H", k)))
                            else:
                                ss.append(mm_job(16, lambda k, wi=wi: W[:, wi, k, :],
                                                 lambda k, n: H[:, k, n * Th:(n + 1) * Th], HK + [("W", wi)], Th))
                        ug = nxt('ugv', 2)
                        U = UGVb(ug)
                        tg = [nxt('tmpf', NTMP), nxt('tmpf', NTMP)]
                        for gv in range(2):
                            P.op('dve', lambda v, gv=gv, U=U, s_=ss[gv]: v.tensor_tensor(
                                out=v3(U[:, gv, 2:2 + Tt]), in0=psv(s_, Th), in1=v3(RSTD[:, 0:Tt]), op=ALU.mult),
                                reads=[("ps", ss[gv]), ("RSTD",)], writes=[("UGV", ug, gv)])
                        for gv in range(2):
                            P.op('dve', lambda v, gv=gv, U=U, t_=tg[gv], f=f: v.tensor_scalar(
                                out=TMPFb(t_, Tt), in0=U[:, gv, 2:2 + Tt], scalar1=pb(328 + 2 * 88 + gv * CF + f), scalar2=0.0,
                                op0=ALU.mult, op1=ALU.add),
                                reads=[("UGV", ug, gv), ("PB",)], writes=[("TMPF", tg[gv])])
                        dk = [("UGV", ug, 0), ("UGV", ug, 1)]
                        for si, seg in enumerate(segs):
                            col, n = seg['col'], seg['n']
                            if seg['kind'] == 'p' and seg['first']:
                                srcap, rk = ZERO[:, 0:4].rearrange("p (g t) -> p g t", g=2), [("ZERO",)]
                            elif seg['kind'] == 'p':
                                srcap, rk = HISTF[:, l, :, f, :], [("HISTF",)]
                            else:
                                srcap, rk = SFT[:, l].rearrange("p t (g f) -> p g t f", g=2)[:, :, :, f], [("SFT", l)]
                            act_copy(U[:, :, col:col + 2], srcap, reads=rk + (dk if col > 0 else []), writes=[("UGVH", ug)])
                            tail = U[:, :, col + n:col + n + 2]
                            if seg['kind'] == 'p' and not seg['last']:
                                act_copy(HISTF[:, l, :, f, :], tail, reads=dk, writes=[("HISTF",)])
                            if seg['last']:
                                act_copy(STF[:, si].rearrange("p t (g f) -> p g t f", g=2)[:, :, :, f], tail,
                                         reads=dk, writes=[("STF", si)])
                        for k in (1, 0):
                            for gv in range(2):
                                P.op('dve', lambda v, gv=gv, U=U, t_=tg[gv], f=f, k=k: v.scalar_tensor_tensor(
                                    out=TMPFb(t_, Tt), in0=U[:, gv, k:k + Tt], scalar=pb(328 + k * 88 + gv * CF + f),
                                    in1=TMPFb(t_, Tt), op0=ALU.mult, op1=ALU.add),
                                    reads=[("UGV", ug, gv), ("UGVH", ug), ("TMPF", tg[gv]), ("PB",)], writes=[("TMPF", tg[gv])])
                        sl_ = nxt('tmpf', NTMP)
                        P.op('act', lambda a, sl_=sl_, t_=tg[0]: a.activation(out=TMPFb(sl_, Tt), in_=TMPFb(t_, Tt), func=AF.Silu),
                             reads=[("TMPF", tg[0])], writes=[("TMPF", sl_)])
                        P.op('dve', lambda v, sl_=sl_, t_=tg[1], bq=bq, i=i: v.tensor_tensor(
                            out=ACTQc(bq, i, 0, Tt), in0=TMPFb(sl_, Tt), in1=TMPFb(t_, Tt), op=ALU.mult),
                            reads=[("TMPF", sl_), ("TMPF", tg[1])], writes=[("ACTQ", bq, i)])

                def down_quarter(q, with_stats=False):
                    bq = q % 2
                    wd_v = w_down[l][q * 1408:(q + 1) * 1408, :].rearrange("(kc p) n -> p kc n", p=128)
                    AK = [("ACTQ", bq, i) for i in range(11)]
                    st3 = rms_begin() if with_stats else None
                    pend = []
                    for j in range(DC):
                        wi = wload(wd_v[:, :, j * 128:(j + 1) * 128], 11)
                        s_ = mm_job(11, lambda k, wi=wi: W[:, wi, k, :],
                                    lambda k, n: ACTQc(bq, k, n * Th, (n + 1) * Th), AK + [("W", wi)], Th)
                        P.op('dve', lambda v, s_=s_, j=j: v.tensor_tensor(
                            out=v3(X[:, j, 0:Tt]), in0=psv(s_, Th), in1=v3(X[:, j, 0:Tt]), op=ALU.add),
                            reads=[("ps", s_), ("X", j)], writes=[("X", j)])
                        if with_stats:
                            pend.append(j)
                            if len(pend) > 2:
                                rms_chunk(st3, pend.pop(0), Tt, Th)
                    if with_stats:
                        for j in pend:
                            rms_chunk(st3, j, Tt, Th)
                        rms_end(st3, Tt, Th, AF.Sigmoid if l < n_layers - 1 else None)
                        nready['v'] = True

                up_quarter(0)
                up_quarter(1)
                down_quarter(0)
                up_quarter(2)
                down_quarter(1)
                up_quarter(3)
                if last_tile:
                    for si, seg in enumerate(segs):
                        nf_dst = nfp if seg['kind'] == 'p' else nfs
                        s_ = nxt('ps', NPS)

                        def emit(t, s_=s_, si=si):
                            for tt in range(2):
                                ins = t.transpose(PS[0:88, 2 * s_, tt * 128:(tt + 1) * 128], STF[:, si, tt, :], IDENT[:, :])
                            return ins
                        P.op('pe', emit, reads=[("STF", si), ("IDENT",)], writes=[("ps", s_)])
                        act_copy(OUTF[:, si].rearrange("c t n -> c (t n)"), PS[0:88, 2 * s_, 0:256],
                                 reads=[("ps", s_)], writes=[("OUTF", si)])
                        P.op('sp', lambda s, si=si, nf_dst=nf_dst: s.dma_start(
                            out=nf_dst[l].rearrange("t (c n) -> c t n", n=128), in_=OUTF[:, si]),
                            reads=[("OUTF", si)], dma="OF%d" % si, is_output=True)
                down_quarter(2)
                down_quarter(3, with_stats=True)

            for l in range(n_layers):
                do_layer(l)

            if not nready['v']:
                rms_stats(Tt, Th)
            nready['v'] = False
            rms_apply(Tt, lambda c: PBF[:, c:c + 1], lambda c: X[:, c, 0:Tt], lambda c: ("X", c))
            for seg in segs:
                dst = yp if seg['kind'] == 'p' else ys
                a = 0
                while a < seg['n']:
                    nb = min(128, seg['n'] - a)
                    tok = seg['tok0'] + a
                    col = seg['col'] + a
                    yo = nxt('yout', 2)
                    for half in range(2):
                        s_ = nxt('ps', NPS)

                        def emit(t, s_=s_, nb=nb, half=half, col=col):
                            for j in range(8):
                                cc = half * 8 + j
                                ins = t.transpose(PS[0:nb, 2 * s_ + j // 4, (j % 4) * 128:(j % 4 + 1) * 128],
                                                  X[:, cc, col:col + nb], IDENT[:, :])
                            return ins
                        P.op('pe', emit, reads=XK + [("IDENT",)], writes=[("ps", s_)])
                        copy_alt(YOUTb(yo)[0:nb, half * 1024:(half + 1) * 1024].rearrange("p (a n) -> p a n", a=2),
                                 PS[0:nb, 2 * s_:2 * s_ + 2, :], reads=[("ps", s_)], writes=[("YOUT", yo, half)])
                    P.op('sp', lambda s, yo=yo, nb=nb, tok=tok, dst=dst:
                         s.dma_start(out=dst[tok:tok + nb, :], in_=YOUTb(yo)[0:nb, :]),
                         reads=[("YOUT", yo, 0), ("YOUT", yo, 1)], dma="YO%d" % yo, is_output=True)
                    a += nb

        for ti, tile in enumerate(TILES):
            if ti == len(TILES) - 1:
                load_states()
            do_tile(ti, tile)


    P0 = Prog()
    slabs = []
    emit_all(P0, slabs, True)
    P = Prog()
    emit_all(P, slabs, False)

    esem = {e: es.enter_context(nc.semaphore("sem_" + e)) for e in ENGS}
    dsem = {n: es.enter_context(nc.semaphore("dsem_" + n)) for n in sorted(P.dcnt)}
    with nc.Block() as block:
        @block.tensor
        def _(t):
            P.replay('pe', t, esem, dsem)

        @block.scalar
        def _(a):
            P.replay('act', a, esem, dsem)

        @block.vector
        def _(v):
            P.replay('dve', v, esem, dsem)

        @block.gpsimd
        def _(g):
            P.replay('pool', g, esem, dsem)

        @block.sync
        def _(s):
            P.replay('sp', s, esem, dsem)
    es.close()
    return nc


def make_in_maps(inp, n_cores=8):
    f = lambda a: np.ascontiguousarray(np.asarray(a, dtype=np.float32))
    shared = dict(
        n1g=f(inp["norm1_g"]), w_in=f(inp["w_in"]), caw=f(inp["conv_a_w"]), cab=f(inp["conv_a_b"]),
        lng=f(inp["ln_a_g"]), lnb=f(inp["ln_a_b"]), cbw=f(inp["conv_b_w"]), w_out=f(inp["w_out"]),
        n2g=f(inp["norm2_g"]), w_up=f(inp["w_up"]), cfw=f(inp["conv_ffn_w"]), w_down=f(inp["w_down"]),
        fg=f(inp["final_g"]))
    xp = f(inp["x_prompt"]); xs = f(inp["x_sample"])
    sa = f(inp["state_conv_a"]); sb = f(inp["state_conv_b"]); sf = f(inp["state_ffn"])
    maps = []
    for b in range(n_cores):
        m = dict(shared)
        m.update(xp=f(xp[b]), xs=f(xs[b]), sa=f(sa[:, b]), sb=f(sb[:, b]), sf=f(sf[:, b]))
        maps.append(m)
    return maps


def kernel(**inputs):
    nc = build_program()
    maps = make_in_maps(inputs)
    res = run_bass_kernel_spmd(nc, maps, core_ids=list(range(8)))
    R = res.results
    st = lambda name, ax: np.stack([np.asarray(R[b][name], dtype=np.float32) for b in range(8)], axis=ax)
    return (st("yp", 0), st("ys", 0), st("nap", 1), st("nbp", 1), st("nfp", 1),
            st("nas", 1), st("nbs", 1), st("nfs", 1))
```

```python
import numpy as np
from contextlib import ExitStack
import concourse.bass as bass
import concourse.mybir as mybir
from concourse.bass_utils import run_bass_kernel_spmd

F32 = mybir.dt.float32
BF16 = mybir.dt.bfloat16
AF = mybir.ActivationFunctionType
ALU = mybir.AluOpType

L = 4
D = 2048
DC = 16
CA = 8
CF = 44
DFF = 5632
HA = 30
KA = 31
TMAX = 704
EPS = 1e-6
NW = 6
NPS = 4
NTMP = 6
NPT = 0
PBW = 640

TILES = [
    dict(T=704, segs=[dict(kind='p', tok0=0, n=704, col=0, first=True, last=False)]),
    dict(T=704, segs=[dict(kind='p', tok0=704, n=704, col=0, first=False, last=False)]),
    dict(T=704, segs=[dict(kind='p', tok0=1408, n=640, col=0, first=False, last=True),
                      dict(kind='s', tok0=0, n=32, col=670, first=False, last=True)],
         gaps=[(640, 670), (702, 704)]),
]

ENGS = ['pe', 'act', 'dve', 'pool', 'sp']


class Prog:
    def __init__(self):
        self.q = {e: [] for e in ENGS}
        self.cnt = {e: 0 for e in ENGS}
        self.dcnt = {}
        self.lastw = {}
        self.readers = {}
        self.kview = {}
        self.rview = {}
        self.rkeys = {}
        self.rfence = {}
        self.out_tokens = []

    @staticmethod
    def _compress(toks):
        m = {}
        for t in toks:
            k = (t[0], t[1])
            if m.get(k, 0) < t[2]:
                m[k] = t[2]
        return [(a, b, n) for (a, b), n in m.items()]

    def _touch(self, k, waits):
        rv = self.kview.get(k[0])
        if rv is None:
            return
        reg, view = rv
        if self.rview.get(reg) != view:
            fence = []
            for kk in self.rkeys.get(reg, ()):
                if kk in self.lastw:
                    fence.append(self.lastw.pop(kk))
                fence.extend(self.readers.pop(kk, []))
            self.rfence[reg] = self._compress(fence)
            self.rview[reg] = view
            self.rkeys[reg] = set()
        self.rkeys[reg].add(k)
        waits.extend(self.rfence.get(reg, []))

    def op(self, eng, emit, reads=(), writes=(), dma=None, is_output=False):
        raw = []
        other = []
        for k in reads:
            self._touch(k, other)
            w = self.lastw.get(k)
            if w is not None:
                raw.append(w)
        for k in writes:
            self._touch(k, other)
            w = self.lastw.get(k)
            if w is not None:
                other.append(w)
            other.extend(self.readers.get(k, []))
        waits = list(raw)
        for t in other:
            if t[0] == 'e' and t[1] == eng:
                continue
            waits.append(t)
        if eng == 'pe':
            waits = [t for t in waits if not (t[0] == 'e' and t[1] == 'pe')]
        waits = self._compress(waits)
        if dma is None:
            self.cnt[eng] += 1
            tok = ('e', eng, self.cnt[eng])
        else:
            self.dcnt[dma] = self.dcnt.get(dma, 0) + 1
            tok = ('d', dma, self.dcnt[dma])
        for k in reads:
            self.readers.setdefault(k, []).append(tok)
        for k in writes:
            self.lastw[k] = tok
            self.readers[k] = []
        self.q[eng].append((waits, emit, tok))
        if is_output:
            self.out_tokens.append(tok)
        return tok

    def finalize(self):
        waited = {e: set() for e in ENGS}
        for e in ENGS:
            for waits, emit, tok in self.q[e]:
                for t in waits:
                    if t[0] == 'e':
                        waited[t[1]].add(t[2])
        for t in self.out_tokens:
            if t[0] == 'e':
                waited[t[1]].add(t[2])
        self.newval = {e: {n: i + 1 for i, n in enumerate(sorted(waited[e]))} for e in ENGS}
        self.n_signalled = {e: len(waited[e]) for e in ENGS}

    def replay(self, eng, engine, esem, dsem):
        seen = {}

        def do_wait(t):
            if t[0] == 'e':
                sem, val = esem[t[1]], self.newval[t[1]][t[2]]
            else:
                sem, val = dsem[t[1]], 16 * t[2]
            key = (t[0], t[1])
            if seen.get(key, 0) < val:
                engine.wait_ge(sem, val)
                seen[key] = val

        for waits, emit, tok in self.q[eng]:
            for t in waits:
                do_wait(t)
            ins = emit(engine)
            if tok[0] == 'e':
                if tok[2] in self.newval[eng]:
                    ins.then_inc(esem[eng], 1)
            else:
                ins.then_inc(dsem[tok[1]], 16)
        if eng == 'sp':
            for t in self._compress(self.out_tokens):
                do_wait(t)


def build_program(n_layers=L):
    nc = bass.Bass("TRN2", target_bir_lowering=False)

    def din(name, shape):
        return nc.dram_tensor(name, shape, F32, kind="ExternalInput").ap()

    def dout(name, shape):
        return nc.dram_tensor(name, shape, F32, kind="ExternalOutput").ap()

    xp = din("xp", [2048, D]); xs = din("xs", [32, D])
    sa = din("sa", [L, 30, 1024]); sbd = din("sb", [L, 2, 1024]); sfd = din("sf", [L, 2, 2 * DFF])
    n1g = din("n1g", [L, D]); w_in = din("w_in", [L, D, 5120]); caw = din("caw", [L, KA, 1024])
    cab = din("cab", [L, 1024]); lng = din("lng", [L, 1024]); lnb = din("lnb", [L, 1024])
    cbw = din("cbw", [L, 3, 1024]); w_out = din("w_out", [L, D, D]); n2g = din("n2g", [L, D])
    w_up = din("w_up", [L, D, 2 * DFF]); cfw = din("cfw", [L, 3, 2 * DFF]); w_down = din("w_down", [L, DFF, D])
    fg = din("fg", [D])
    yp = dout("yp", [2048, D]); ys = dout("ys", [32, D])
    nap = dout("nap", [L, 30, 1024]); nbp = dout("nbp", [L, 2, 1024]); nfp = dout("nfp", [L, 2, 2 * DFF])
    nas = dout("nas", [L, 30, 1024]); nbs = dout("nbs", [L, 2, 1024]); nfs = dout("nfs", [L, 2, 2 * DFF])

    es = ExitStack()

    def sb(name, shape, dt=F32):
        return es.enter_context(nc.sbuf_tensor(name, shape, dt))

    X = sb("X", [128, DC, TMAX])
    H = sb("H", [128, DC, TMAX], BF16)
    PW = 8448 + NTMP * TMAX + 2 * 734 + 2 * 706
    PR = sb("PR", [128, PW])
    SQ = sb("SQ", [128, 3, TMAX], BF16)
    SQRT = sb("SQRT", [128, TMAX]); RSTD = sb("RSTD", [128, TMAX])
    MU = sb("MU", [128, TMAX]); VAR = sb("VAR", [128, TMAX])
    W = sb("W", [128, NW, 16, 128], BF16)
    PB = sb("PB", [128, L, PBW]); PBF = sb("PBF", [128, 16]); PSTF = sb("PSTF", [16, 128])
    HISTA = sb("HISTA", [128, L, CA, HA]); HISTB = sb("HISTB", [128, L, CA, 2])
    HISTF = sb("HISTF", [128, L, 2, CF, 2])
    SAT = sb("SAT", [128, L, 2, HA, 4]); SBT = sb("SBT", [128, L, 2, CA]); SFT = sb("SFT", [128, L, 2, 2 * CF])
    SA_in = sb("SA_in", [120, 256]); SB_in = sb("SB_in", [16, 128]); SF_in = sb("SF_in", [88, 2, 128])
    STA = sb("STA", [128, 2, 2, HA, 4]); STB = sb("STB", [128, 2, 2, CA]); STF = sb("STF", [128, 2, 2, 2 * CF])
    OUTA = sb("OUTA", [120, 2, 256]); OUTB = sb("OUTB", [16, 2, 128]); OUTF = sb("OUTF", [88, 2, 2, 128])
    IDENT = sb("IDENT", [128, 128]); ONESB = sb("ONESB", [128, 128], BF16); ONESB2 = sb("ONESB2", [128, 128], BF16)
    PST = sb("PST", [128, 5, 128])
    ZERO = sb("ZERO", [128, 32]); EPSC = sb("EPSC", [128, 1]); DUMMY = sb("DUMMY", [128, 4])
    PS = es.enter_context(nc.psum_tensor("PS", [128, 2 * NPS, 512], F32))

    def emit_all(P, slabs, dry):
        for k in ("R", "MB", "TMP", "AGLU", "AGLUH", "CH", "CHH"):
            P.kview[k] = ("PR", "mix")
        for k in ("ACTQ", "UGV", "UGVH", "TMPF"):
            P.kview[k] = ("PR", "ffn")
        P.kview["XIN"] = ("PR", "io_in")
        P.kview["YOUT"] = ("PR", "io_out")

        R_bf = PR[:, 0:5632].bitcast(BF16)
        MB_bf = PR[:, 5632:8448].bitcast(BF16)
        ACTQ_bf = PR[:, 0:7744].bitcast(BF16)

        def ACONV(c, Tt): return PR[:, c * TMAX: c * TMAX + Tt]
        def MAc(c, a, b): return R_bf[:, c * TMAX + a: c * TMAX + b]
        def MBc(c, a, b): return MB_bf[:, c * TMAX + a: c * TMAX + b]
        def TMPb(i, Tt): return PR[:, 8448 + i * TMAX: 8448 + i * TMAX + Tt]
        def TMPbf(i, a, b): return PR[:, 8448 + i * TMAX: 8448 + (i + 1) * TMAX].bitcast(BF16)[:, a:b]
        AG0 = 8448 + NTMP * TMAX
        def AGLUb(i, a, b): return PR[:, AG0 + i * 734 + a: AG0 + i * 734 + b]
        CH0 = AG0 + 2 * 734
        def CHb(i, a, b): return PR[:, CH0 + i * 706 + a: CH0 + i * 706 + b]
        def ACTQc(bq, i, a, b): return ACTQ_bf[:, (bq * 11 + i) * TMAX + a: (bq * 11 + i) * TMAX + b]
        def UGVb(i): return PR[:, 7744 + i * 1412: 7744 + (i + 1) * 1412].rearrange("p (g t) -> p g t", g=2)
        def TMPFb(i, Tt): return PR[:, 10568 + i * TMAX: 10568 + i * TMAX + Tt]
        def XINb(i): return PR[:, i * 2048:(i + 1) * 2048]
        def YOUTb(i): return PR[:, i * 2048:(i + 1) * 2048]

        def v3(ap):
            return ap.rearrange("p (a n) -> p a n", a=2)

        ctr = dict(ps=0, w=0, tmp=0, sq=0, aglu=0, ch=0, ugv=0, tmpf=0, xin=0, yout=0, alt=0, alt2=0)

        ps_reserved = set()

        def nxt(name, n):
            while True:
                v = ctr[name]
                ctr[name] = (v + 1) % n
                if name != 'ps' or v not in ps_reserved:
                    return v

        def psv(s, Th):
            return PS[:, 2 * s:2 * s + 2, 0:Th]

        P.op('pool', lambda g: g.memset(IDENT[:], 0.0), writes=[("IDENT",)])
        P.op('pool', lambda g: g.affine_select(out=IDENT[:], in_=IDENT[:], pattern=[[-1, 128]],
                                               compare_op=ALU.not_equal, fill=1.0, base=0, channel_multiplier=1),
             reads=[("IDENT",)], writes=[("IDENT",)])
        P.op('dve', lambda v: v.memset(ONESB[:], 1.0 / D), writes=[("ONESB",)])
        P.op('dve', lambda v: v.memset(ONESB2[:], 1.0 / 1024), writes=[("ONESB2",)])
        P.op('dve', lambda v: v.memset(ZERO[:], 0.0), writes=[("ZERO",)])
        P.op('dve', lambda v: v.memset(EPSC[:], EPS), writes=[("EPSC",)])

        def copy_alt(out, in_, reads, writes):
            if nxt('alt', 2) == 0:
                P.op('act', lambda a: a.activation(out=out, in_=in_, func=AF.Copy), reads=reads, writes=writes)
            else:
                P.op('dve', lambda v: v.tensor_copy(out=out, in_=in_), reads=reads, writes=writes)

        def act_copy(out, in_, reads, writes):
            P.op('act', lambda a: a.activation(out=out, in_=in_, func=AF.Copy), reads=reads, writes=writes)

        def load_params():
            def rows2(ap1d_or_2d, pat, **kw):
                return ap1d_or_2d.rearrange(pat, **kw)

            for l in range(n_layers):
                pieces = [
                    (rows2(n1g[l], "(c n) -> c n", n=128), 0, 16),
                    (rows2(n2g[l], "(c n) -> c n", n=128), 16, 16),
                    (rows2(cab[l], "(c n) -> c n", n=128), 32, 8),
                    (rows2(lng[l], "(c n) -> c n", n=128), 40, 8),
                    (rows2(lnb[l], "(c n) -> c n", n=128), 48, 8),
                    (rows2(cbw[l], "k (c n) -> (k c) n", n=128), 56, 24),
                    (rows2(caw[l], "k (c n) -> (k c) n", n=128), 80, 248),
                    (rows2(cfw[l], "k (c n) -> (k c) n", n=128), 328, 264),
                ]
                for src, r0, nr in pieces:
                    a = 0
                    while a < nr:
                        g = (r0 + a) // 128
                        p0 = (r0 + a) % 128
                        m = min(nr - a, 128 - p0)
                        P.op('act' if nxt('alt2', 2) == 0 else 'sp', lambda s, src=src, a=a, m=m, g=g, p0=p0:
                             s.dma_start(out=PST[p0:p0 + m, g, :], in_=src[a:a + m, :]),
                             writes=[("PST", g)], dma="par")
                        a += m
                s_ = nxt('ps', NPS)

                def emit_pt(t, s_=s_):
                    for g in range(5):
                        rows = min(128, 592 - g * 128)
                        ins = t.transpose(PS[:, 2 * s_ + g // 4, (g % 4) * 128:(g % 4) * 128 + rows],
                                          PST[0:rows, g, :], IDENT[0:rows, 0:rows])
                    return ins
                P.op('pe', emit_pt, reads=[("PST", g) for g in range(5)] + [("IDENT",)], writes=[("ps", s_)])
                P.op('dve', lambda v, s_=s_, l=l: v.tensor_copy(out=PB[:, l, 0:512], in_=PS[:, 2 * s_, 0:512]),
                     reads=[("ps", s_)], writes=[("PB",)])
                P.op('dve', lambda v, s_=s_, l=l: v.tensor_copy(out=PB[:, l, 512:592], in_=PS[:, 2 * s_ + 1, 0:80]),
                     reads=[("ps", s_)], writes=[("PB",)])
            P.op('sp', lambda s: s.dma_start(out=PSTF[:, :], in_=fg.rearrange("(c n) -> c n", n=128)),
                 writes=[("PSTF",)], dma="parf")
            s_ = nxt('ps', NPS)
            P.op('pe', lambda t, s_=s_: t.transpose(PS[:, 2 * s_, 0:16], PSTF[0:16, :], IDENT[0:16, 0:16]),
                 reads=[("PSTF",), ("IDENT",)], writes=[("ps", s_)])
            P.op('dve', lambda v, s_=s_: v.tensor_copy(out=PBF[:, :], in_=PS[:, 2 * s_, 0:16]),
                 reads=[("ps", s_)], writes=[("PBF",)])


        def load_states():
            for l in range(n_layers):
                P.op('sp', lambda s, l=l: s.dma_start(out=SA_in[:, :], in_=sa[l].rearrange("t (q n) -> (t q) n", n=256)),
                     writes=[("SA_in",)], dma="sa_in")
                s_ = nxt('ps', NPS)

                def emit_sa(t, s_=s_):
                    for j in range(2):
                        ins = t.transpose(PS[:, 2 * s_, j * 128:j * 128 + 120], SA_in[0:120, j * 128:(j + 1) * 128],
                                          IDENT[0:120, 0:120])
                    return ins
                P.op('pe', emit_sa, reads=[("SA_in",), ("IDENT",)], writes=[("ps", s_)])
                P.op('act', lambda a, s_=s_, l=l: a.activation(
                    out=SAT[:, l].rearrange("p j t q -> p j (t q)"),
                    in_=PS[:, 2 * s_, 0:256].rearrange("p (j n) -> p j n", j=2)[:, :, 0:120], func=AF.Copy),
                    reads=[("ps", s_)], writes=[("SAT", l)])
                P.op('sp', lambda s, l=l: s.dma_start(out=SB_in[:, :], in_=sbd[l].rearrange("t (c n) -> (t c) n", n=128)),
                     writes=[("SB_in",)], dma="sb_in")
                s_ = nxt('ps', NPS)
                P.op('pe', lambda t, s_=s_: t.transpose(PS[:, 2 * s_, 0:16], SB_in[0:16, :], IDENT[0:16, 0:16]),
                     reads=[("SB_in",), ("IDENT",)], writes=[("ps", s_)])
                P.op('act', lambda a, s_=s_, l=l: a.activation(
                    out=SBT[:, l].rearrange("p t c -> p (t c)"), in_=PS[:, 2 * s_, 0:16], func=AF.Copy),
                    reads=[("ps", s_)], writes=[("SBT", l)])
                P.op('sp', lambda s, l=l: s.dma_start(out=SF_in[:, :, :], in_=sfd[l].rearrange("t (c n) -> c t n", n=128)),
                     writes=[("SF_in",)], dma="sf_in")
                s_ = nxt('ps', NPS)

                def emit_sf(t, s_=s_):
                    for tt in range(2):
                        ins = t.transpose(PS[:, 2 * s_, tt * 128:tt * 128 + 88], SF_in[0:88, tt, :], IDENT[0:88, 0:88])
                    return ins
                P.op('pe', emit_sf, reads=[("SF_in",), ("IDENT",)], writes=[("ps", s_)])
                P.op('act', lambda a, s_=s_, l=l: a.activation(
                    out=SFT[:, l], in_=PS[:, 2 * s_, 0:256].rearrange("p (t n) -> p t n", t=2)[:, :, 0:88], func=AF.Copy),
                    reads=[("ps", s_)], writes=[("SFT", l)])


        wst = dict(n=0, issued=0)

        def wload(src, nk):
            n = wst['n']
            wst['n'] += 1
            if dry:
                slabs.append((src, nk))
                return n % NW
            upto = min(n + NW - 1, len(slabs) - 1)
            while wst['issued'] <= upto:
                m = wst['issued']
                s_src, s_nk = slabs[m]
                i = m % NW
                P.op('pool', lambda g, i=i, s_src=s_src, s_nk=s_nk: g.dma_start(out=W[:, i, 0:s_nk, :], in_=s_src),
                     writes=[("W", i)], dma="W%d" % i)
                wst['issued'] += 1
            return n % NW

        def mm_job(nk, lhsT_fn, rhs_fn, reads, Th, fine=None):
            s = nxt('ps', NPS)
            if fine is not None:
                for k in range(nk):
                    def emit_k(t, k=k):
                        lt = lhsT_fn(k)
                        for n in range(2):
                            ins = t.matmul(PS[:, 2 * s + n, 0:Th], lhsT=lt, rhs=rhs_fn(k, n),
                                           start=(k == 0), stop=(k == nk - 1))
                        return ins
                    P.op('pe', emit_k, reads=[fine(k)] + reads, writes=[("ps", s)])
                return s

            def emit(t):
                ins = None
                for k in range(nk):
                    lt = lhsT_fn(k)
                    for n in range(2):
                        ins = t.matmul(PS[:, 2 * s + n, 0:Th], lhsT=lt, rhs=rhs_fn(k, n),
                                       start=(k == 0), stop=(k == nk - 1))
                return ins
            P.op('pe', emit, reads=reads, writes=[("ps", s)])
            return s

        XK = [("X", c) for c in range(DC)]
        HK = [("H", c) for c in range(DC)]

        def act_preload(func, col):
            P.op('act', lambda a: a.activation(out=DUMMY[:, col:col + 1], in_=EPSC[:, 0:1], func=func),
                 reads=[("EPSC",)], writes=[("DUMMY", col)])

        def rms_begin():
            s = nxt('ps', NPS)
            ps_reserved.add(s)
            return dict(s=s)

        def rms_chunk(st, c, Tt, Th):
            s = st['s']
            i = nxt('sq', 3)
            P.op('act', lambda a, i=i, c=c: a.activation(out=SQ[:, i, 0:Tt], in_=X[:, c, 0:Tt], func=AF.Square),
                 reads=[("X", c)], writes=[("SQ", i)])

            def emit(t, i=i, c=c):
                for n in range(2):
                    ins = t.matmul(PS[:, 2 * s + n, 0:Th], lhsT=ONESB[:, :], rhs=SQ[:, i, n * Th:(n + 1) * Th],
                                   start=(c == 0), stop=(c == DC - 1))
                return ins
            P.op('pe', emit, reads=[("SQ", i), ("ONESB",)], writes=[("ps", s)])

        def rms_end(st, Tt, Th, next_func=None):
            s = st['s']
            act_preload(AF.Sqrt, 0)
            P.op('act', lambda a: a.activation(out=v3(SQRT[:, 0:Tt]), in_=psv(s, Th), func=AF.Sqrt, bias=EPSC[:, 0:1]),
                 reads=[("ps", s), ("EPSC",)], writes=[("SQRT",)])
            ps_reserved.discard(s)
            if next_func is not None:
                act_preload(next_func, 1)
            P.op('dve', lambda v: v.reciprocal(out=RSTD[:, 0:Tt], in_=SQRT[:, 0:Tt]),
                 reads=[("SQRT",)], writes=[("RSTD",)])

        def rms_stats(Tt, Th, next_func=None):
            st = rms_begin()
            for c in range(DC):
                rms_chunk(st, c, Tt, Th)
            rms_end(st, Tt, Th, next_func)

        def rms_apply(Tt, gcol, out_fn, out_key):
            for c in range(DC):
                P.op('dve', lambda v, c=c: v.scalar_tensor_tensor(
                    out=out_fn(c), in0=X[:, c, 0:Tt], scalar=gcol(c), in1=RSTD[:, 0:Tt], op0=ALU.mult, op1=ALU.mult),
                    reads=[("X", c), ("RSTD",), ("PB",), ("PBF",)], writes=[out_key(c)])

        def do_tile(ti, tile):
            Tt = tile['T']
            Th = Tt // 2
            segs = tile['segs']

            for g0, g1 in tile.get('gaps', []):
                P.op('dve', lambda v, g0=g0, g1=g1: v.memset(X[:, :, g0:g1], 0.0), writes=XK)
            for seg in segs:
                src = xp if seg['kind'] == 'p' else xs
                a = 0
                while a < seg['n']:
                    nb = min(128, seg['n'] - a)
                    tok = seg['tok0'] + a
                    col = seg['col'] + a
                    xi = nxt('xin', 2)
                    P.op('sp', lambda s, xi=xi, nb=nb, tok=tok, src=src:
                         s.dma_start(out=XINb(xi)[0:nb, :], in_=src[tok:tok + nb, :]),
                         writes=[("XIN", xi)], dma="XIN%d" % xi)
                    for half in range(2):
                        s_ = nxt('ps', NPS)

                        def emit(t, s_=s_, xi=xi, nb=nb, half=half):
                            for j in range(8):
                                cc = half * 8 + j
                                ins = t.transpose(PS[:, 2 * s_ + j // 4, (j % 4) * 128:(j % 4) * 128 + nb],
                                                  XINb(xi)[0:nb, cc * 128:(cc + 1) * 128], IDENT[0:nb, 0:nb])
                            return ins
                        P.op('pe', emit, reads=[("XIN", xi), ("IDENT",)], writes=[("ps", s_)])
                        copy_alt(X[:, half * 8:half * 8 + 8, col:col + nb],
                                 PS[:, 2 * s_:2 * s_ + 2, :].rearrange("p a (b n) -> p (a b) n", n=128)[:, :, 0:nb],
                                 reads=[("ps", s_)], writes=[("X", half * 8 + j) for j in range(8)])
                    a += nb

            if ti == 0:
                load_params()
            nready = dict(v=False)

            def do_layer(l):
                def pb(col):
                    return PB[:, l, col:col + 1]

                if not nready['v']:
                    rms_stats(Tt, Th, AF.Sigmoid)
                nready['v'] = False
                rms_apply(Tt, lambda c: pb(c), lambda c: H[:, c, 0:Tt], lambda c: ("H", c))

                w_in_v = w_in[l].rearrange("(kc p) n -> p kc n", p=128)

                def inproj(group, c):
                    col0 = group * 1024 + c * 128
                    wi = wload(w_in_v[:, :, col0:col0 + 128], 16)
                    if False and c == 0 and group in (0, 1):
                        return mm_job(16, lambda k, wi=wi: W[:, wi, k, :],
                                      lambda k, n: H[:, k, n * Th:(n + 1) * Th], [("W", wi)], Th,
                                      fine=lambda k: ("H", k))
                    return mm_job(16, lambda k, wi=wi: W[:, wi, k, :],
                                  lambda k, n: H[:, k, n * Th:(n + 1) * Th], HK + [("W", wi)], Th)

                def hist_ops(seg_list, H_, bufslice, hkey, dkey, zero_src, hist_src, state_src, hist_dst, st_dst, l=l):
                    for si, seg in enumerate(seg_list):
                        col, n = seg['col'], seg['n']
                        if seg['kind'] == 'p' and seg['first']:
                            srcap, rk = zero_src, [("ZERO",)]
                        elif seg['kind'] == 'p':
                            srcap, rk = hist_src, [("HIST",)]
                        else:
                            srcap, rk = state_src, [("SAT", l), ("SBT", l), ("SFT", l)]
                        act_copy(bufslice(col, col + H_), srcap, reads=rk + ([dkey] if col > 0 else []), writes=[hkey])
                        tail = bufslice(col + n, col + n + H_)
                        if seg['kind'] == 'p' and not seg['last']:
                            act_copy(hist_dst, tail, reads=[dkey], writes=[("HIST",)])
                        if seg['last']:
                            act_copy(st_dst(si), tail, reads=[dkey], writes=[("ST", si)])

                last_tile = segs[0]['last']

                acc1_slot = [None]

                def doA(c, mid_cb=None):
                    sv = inproj(0, c)
                    sg = inproj(1, c)
                    ts_ = nxt('tmp', NTMP)
                    P.op('act', lambda a, ts_=ts_, sg=sg: a.activation(out=v3(TMPb(ts_, Tt)), in_=psv(sg, Th), func=AF.Sigmoid),
                         reads=[("ps", sg)], writes=[("TMP", ts_)])
                    ag = nxt('aglu', 2)
                    P.op('dve', lambda v, ag=ag, sv=sv, ts_=ts_: v.tensor_tensor(
                        out=v3(AGLUb(ag, HA, HA + Tt)), in0=psv(sv, Th), in1=v3(TMPb(ts_, Tt)), op=ALU.mult),
                        reads=[("ps", sv), ("TMP", ts_)], writes=[("AGLU", ag)])
                    j_, q_ = c % 2, c // 2
                    hist_ops(segs, HA, lambda a, b, ag=ag: AGLUb(ag, a, b), ("AGLUH", ag), ("AGLU", ag),
                             ZERO[:, 0:HA], HISTA[:, l, c, :], SAT[:, l, j_, :, q_], HISTA[:, l, c, :],
                             lambda si, j_=j_, q_=q_: STA[:, si, j_, :, q_])
                    acc0 = ACONV(c, Tt)
                    t1 = nxt('tmp', NTMP)
                    acc1_slot[0] = t1
                    acc1 = TMPb(t1, Tt)
                    k0 = [("R", 2 * c), ("R", 2 * c + 1)]
                    k1 = [("TMP", t1)]
                    src_keys = [("AGLU", ag), ("AGLUH", ag), ("PB",)]
                    P.op('act', lambda a, ag=ag, acc0=acc0, c=c: a.activation(
                        out=acc0, in_=AGLUb(ag, 0, Tt), func=AF.Identity, scale=pb(80 + c), bias=pb(32 + c)),
                        reads=src_keys, writes=k0)
                    P.op('act', lambda a, ag=ag, acc1=acc1, c=c: a.activation(
                        out=acc1, in_=AGLUb(ag, 1, 1 + Tt), func=AF.Copy, scale=pb(80 + 8 + c)),
                        reads=src_keys, writes=k1)
                    if NPT > 0:
                        a2 = nxt('tmp', NTMP)
                        t2 = nxt('tmp', NTMP)
                        acc2 = TMPb(a2, Tt)
                        tmp2 = TMPb(t2, Tt)
                        for k in range(KA - NPT, KA):
                            if k == KA - NPT:
                                P.op('pool', lambda g, ag=ag, acc2=acc2, c=c, k=k: g.tensor_scalar(
                                    out=acc2, in0=AGLUb(ag, k, k + Tt), scalar1=pb(80 + k * 8 + c), scalar2=None, op0=ALU.mult),
                                    reads=src_keys, writes=[("TMP", a2)])
                            else:
                                P.op('pool', lambda g, ag=ag, tmp2=tmp2, c=c, k=k: g.tensor_scalar(
                                    out=tmp2, in0=AGLUb(ag, k, k + Tt), scalar1=pb(80 + k * 8 + c), scalar2=None, op0=ALU.mult),
                                    reads=src_keys, writes=[("TMP", t2)])
                                P.op('pool', lambda g, acc2=acc2, tmp2=tmp2: g.tensor_tensor(
                                    out=acc2, in0=acc2, in1=tmp2, op=ALU.add),
                                    reads=[("TMP", a2), ("TMP", t2)], writes=[("TMP", a2)])
                    for k in range(2, KA - NPT):
                        if mid_cb is not None and k == 16:
                            mid_cb()
                        acc, kk = (acc0, k0) if k % 2 == 0 else (acc1, k1)
                        P.op('dve', lambda v, ag=ag, acc=acc, c=c, k=k: v.scalar_tensor_tensor(
                            out=acc, in0=AGLUb(ag, k, k + Tt), scalar=pb(80 + k * 8 + c), in1=acc, op0=ALU.mult, op1=ALU.add),
                            reads=src_keys + kk, writes=kk)
                    P.op('dve', lambda v, acc0=acc0, acc1=acc1: v.tensor_tensor(out=acc0, in0=acc0, in1=acc1, op=ALU.add),
                         reads=k0 + k1, writes=k0)
                    if NPT > 0:
                        P.op('dve', lambda v, acc0=acc0, acc2=acc2: v.tensor_tensor(out=acc0, in0=acc0, in1=acc2, op=ALU.add),
                             reads=k0 + [("TMP", a2)], writes=k0)

                def doB(c):
                    sh = inproj(4, c)
                    bh = nxt('tmp', NTMP)
                    P.op('act', lambda a, bh=bh, sh=sh: a.activation(out=v3(TMPb(bh, Tt)), in_=psv(sh, Th), func=AF.Copy),
                         reads=[("ps", sh)], writes=[("TMP", bh)])
                    sc = inproj(3, c)
                    ch = nxt('ch', 2)
                    P.op('dve', lambda v, ch=ch, sc=sc, bh=bh: v.tensor_tensor(
                        out=v3(CHb(ch, 2, 2 + Tt)), in0=psv(sc, Th), in1=v3(TMPb(bh, Tt)), op=ALU.mult),
                        reads=[("ps", sc), ("TMP", bh)], writes=[("CH", ch)])
                    hist_ops(segs, 2, lambda a, b, ch=ch: CHb(ch, a, b), ("CHH", ch), ("CH", ch),
                             ZERO[:, 0:2], HISTB[:, l, c, :], SBT[:, l, :, c], HISTB[:, l, c, :],
                             lambda si, c=c: STB[:, si, :, c])
                    u3 = nxt('tmp', NTMP)
                    srck = [("CH", ch), ("CHH", ch), ("PB",)]
                    P.op('act', lambda a, u3=u3, ch=ch, c=c: a.activation(
                        out=TMPb(u3, Tt), in_=CHb(ch, 0, Tt), func=AF.Copy, scale=pb(56 + c)),
                        reads=srck, writes=[("TMP", u3)])
                    for k in (1, 2):
                        P.op('dve', lambda v, u3=u3, ch=ch, c=c, k=k: v.scalar_tensor_tensor(
                            out=TMPb(u3, Tt), in0=CHb(ch, k, k + Tt), scalar=pb(56 + k * 8 + c), in1=TMPb(u3, Tt),
                            op0=ALU.mult, op1=ALU.add), reads=srck + [("TMP", u3)], writes=[("TMP", u3)])
                    sbb = inproj(2, c)
                    P.op('dve', lambda v, sbb=sbb, u3=u3, c=c: v.tensor_tensor(
                        out=v3(MBc(c, 0, Tt)), in0=psv(sbb, Th), in1=v3(TMPb(u3, Tt)), op=ALU.mult),
                        reads=[("ps", sbb), ("TMP", u3)], writes=[("MB", c)])

                for c in range(CA):
                    if c == CA - 1:
                        doA(c, mid_cb=lambda: doB(CA - 1))
                    else:
                        doA(c)
                        doB(c)

                s_mu = nxt('ps', NPS)
                s_e2 = nxt('ps', NPS)
                for c in range(CA):
                    sq = nxt('tmp', NTMP)
                    if c < CA - 1 and sq == acc1_slot[0]:
                        sq = nxt('tmp', NTMP)
                    kc = [("R", 2 * c), ("R", 2 * c + 1)]
                    P.op('act', lambda a, sq=sq, c=c: a.activation(out=TMPbf(sq, 0, Tt), in_=ACONV(c, Tt), func=AF.Square),
                         reads=kc, writes=[("TMP", sq)])
                    P.op('act', lambda a, sq=sq, c=c: a.activation(out=TMPbf(sq, TMAX, TMAX + Tt), in_=ACONV(c, Tt), func=AF.Copy),
                         reads=kc, writes=[("TMP", sq)])

                    def emit(t, c=c, sq=sq):
                        for n in range(2):
                            t.matmul(PS[:, 2 * s_mu + n, 0:Th], lhsT=ONESB2[:, :], rhs=TMPbf(sq, TMAX + n * Th, TMAX + (n + 1) * Th),
                                     start=(c == 0), stop=(c == CA - 1))
                        for n in range(2):
                            ins = t.matmul(PS[:, 2 * s_e2 + n, 0:Th], lhsT=ONESB2[:, :], rhs=TMPbf(sq, n * Th, (n + 1) * Th),
                                           start=(c == 0), stop=(c == CA - 1))
                        return ins
                    P.op('pe', emit, reads=[("TMP", sq), ("ONESB2",)], writes=[("ps", s_mu), ("ps", s_e2)])
                act_preload(AF.Sqrt, 0)
                P.op('act', lambda a: a.activation(out=v3(MU[:, 0:Tt]), in_=psv(s_mu, Th), func=AF.Copy),
                     reads=[("ps", s_mu)], writes=[("MU",)])
                P.op('dve', lambda v: v.tensor_tensor(out=VAR[:, 0:Tt], in0=MU[:, 0:Tt], in1=MU[:, 0:Tt], op=ALU.mult),
                     reads=[("MU",)], writes=[("VAR",)])
                P.op('dve', lambda v: v.tensor_tensor(out=v3(VAR[:, 0:Tt]), in0=psv(s_e2, Th), in1=v3(VAR[:, 0:Tt]), op=ALU.subtract),
                     reads=[("ps", s_e2), ("VAR",)], writes=[("VAR",)])
                P.op('act', lambda a: a.activation(out=SQRT[:, 0:Tt], in_=VAR[:, 0:Tt], func=AF.Sqrt, bias=EPSC[:, 0:1]),
                     reads=[("VAR",), ("EPSC",)], writes=[("SQRT",)])
                act_preload(AF.Silu, 1)
                P.op('dve', lambda v: v.reciprocal(out=RSTD[:, 0:Tt], in_=SQRT[:, 0:Tt]),
                     reads=[("SQRT",)], writes=[("RSTD",)])
                for c0 in range(0, CA, 2):
                    tt_ = [nxt('tmp', NTMP), nxt('tmp', NTMP)]
                    for j in range(2):
                        c = c0 + j
                        P.op('dve', lambda v, c=c, t_=tt_[j]: v.tensor_tensor(
                            out=TMPb(t_, Tt), in0=ACONV(c, Tt), in1=MU[:, 0:Tt], op=ALU.subtract),
                            reads=[("R", 2 * c), ("R", 2 * c + 1), ("MU",)], writes=[("TMP", tt_[j])])
                    for j in range(2):
                        P.op('dve', lambda v, t_=tt_[j]: v.tensor_tensor(
                            out=TMPb(t_, Tt), in0=TMPb(t_, Tt), in1=RSTD[:, 0:Tt], op=ALU.mult),
                            reads=[("TMP", tt_[j]), ("RSTD",)], writes=[("TMP", tt_[j])])
                    for j in range(2):
                        c = c0 + j
                        P.op('act', lambda a, c=c, t_=tt_[j]: a.activation(
                            out=MAc(c, 0, Tt), in_=TMPb(t_, Tt), func=AF.Silu, scale=pb(40 + c), bias=pb(48 + c)),
                            reads=[("TMP", tt_[j]), ("PB",)], writes=[("R", c)])

                if last_tile:
                    for si, seg in enumerate(segs):
                        na_dst = nap if seg['kind'] == 'p' else nas
                        nb_dst = nbp if seg['kind'] == 'p' else nbs
                        s_ = nxt('ps', NPS)

                        def emit(t, s_=s_, si=si):
                            for j in range(2):
                                ins = t.transpose(PS[0:120, 2 * s_, j * 128:(j + 1) * 128],
                                                  STA[:, si, j].rearrange("p t q -> p (t q)"), IDENT[:, :])
                            return ins
                        P.op('pe', emit, reads=[("ST", si), ("IDENT",)], writes=[("ps", s_)])
                        act_copy(OUTA[:, si, :], PS[0:120, 2 * s_, 0:256], reads=[("ps", s_)], writes=[("OUTA", si)])
                        P.op('sp', lambda s, si=si, na_dst=na_dst: s.dma_start(
                            out=na_dst[l].rearrange("t (q n) -> (t q) n", n=256), in_=OUTA[:, si, :]),
                            reads=[("OUTA", si)], dma="OA%d" % si, is_output=True)
                        s_ = nxt('ps', NPS)
                        P.op('pe', lambda t, s_=s_, si=si: t.transpose(
                            PS[0:16, 2 * s_, 0:128], STB[:, si].rearrange("p t c -> p (t c)"), IDENT[:, :]),
                            reads=[("ST", si), ("IDENT",)], writes=[("ps", s_)])
                        act_copy(OUTB[:, si, :], PS[0:16, 2 * s_, 0:128], reads=[("ps", s_)], writes=[("OUTB", si)])
                        P.op('sp', lambda s, si=si, nb_dst=nb_dst: s.dma_start(
                            out=nb_dst[l].rearrange("t (c n) -> (t c) n", n=128), in_=OUTB[:, si, :]),
                            reads=[("OUTB", si)], dma="OB%d" % si, is_output=True)

                w_out_v = w_out[l].rearrange("(kc p) n -> p kc n", p=128)
                MK = [("R", c) for c in range(CA)] + [("MB", c) for c in range(CA)]

                def m_rhs(k, n):
                    if k < CA:
                        return MAc(k, n * Th, (n + 1) * Th)
                    return MBc(k - CA, n * Th, (n + 1) * Th)
                st2 = rms_begin()
                pend = []
                for j in range(DC):
                    wi = wload(w_out_v[:, :, j * 128:(j + 1) * 128], 16)
                    s_ = mm_job(16, lambda k, wi=wi: W[:, wi, k, :], m_rhs, MK + [("W", wi)], Th)
                    P.op('dve', lambda v, s_=s_, j=j: v.tensor_tensor(
                        out=v3(X[:, j, 0:Tt]), in0=psv(s_, Th), in1=v3(X[:, j, 0:Tt]), op=ALU.add),
                        reads=[("ps", s_), ("X", j)], writes=[("X", j)])
                    pend.append(j)
                    if len(pend) > 2:
                        rms_chunk(st2, pend.pop(0), Tt, Th)
                    P.op('act', lambda a, j=j: a.activation(out=H[:, j, 0:Tt], in_=X[:, j, 0:Tt], func=AF.Copy, scale=pb(16 + j)),
                         reads=[("X", j), ("PB",)], writes=[("H", j)])
                for j in pend:
                    rms_chunk(st2, j, Tt, Th)

                rms_end(st2, Tt, Th, AF.Silu)
                w_up_v = w_up[l].rearrange("(kc p) n -> p kc n", p=128)

                def up_quarter(q):
                    bq = q % 2
                    for i in range(11):
                        f = q * 11 + i
                        ss = []
                        for gv in range(2):
                            col0 = gv * DFF + f * 128
                            wi = wload(w_up_v[:, :, col0:col0 + 128], 16)
                            if False and f == 0:
                                ss.append(mm_job(16, lambda k, wi=wi: W[:, wi, k, :],
                                                 lambda k, n: H[:, k, n * Th:(n + 1) * Th], [("W", wi)], Th,
                                                 fine=lambda k: ("H", k)))
                            else:
                                ss.append(mm_job(16, lambda k, wi=wi: W[:, wi, k, :],
                                                 lambda k, n: H[:, k, n * Th:(n + 1) * Th], HK + [("W", wi)], Th))
                        ug = nxt('ugv', 2)
                        U = UGVb(ug)
                        tg = [nxt('tmpf', NTMP), nxt('tmpf', NTMP)]
                        for gv in range(2):
                            P.op('dve', lambda v, gv=gv, U=U, s_=ss[gv]: v.tensor_tensor(
                                out=v3(U[:, gv, 2:2 + Tt]), in0=psv(s_, Th), in1=v3(RSTD[:, 0:Tt]), op=ALU.mult),
                                reads=[("ps", ss[gv]), ("RSTD",)], writes=[("UGV", ug, gv)])
                        for gv in range(2):
                            P.op('dve', lambda v, gv=gv, U=U, t_=tg[gv], f=f: v.tensor_scalar(
                                out=TMPFb(t_, Tt), in0=U[:, gv, 2:2 + Tt], scalar1=pb(328 + 2 * 88 + gv * CF + f), scalar2=0.0,
                                op0=ALU.mult, op1=ALU.add),
                                reads=[("UGV", ug, gv), ("PB",)], writes=[("TMPF", tg[gv])])
                        dk = [("UGV", ug, 0), ("UGV", ug, 1)]
                        for si, seg in enumerate(segs):
                            col, n = seg['col'], seg['n']
                            if seg['kind'] == 'p' and seg['first']:
                                srcap, rk = ZERO[:, 0:4].rearrange("p (g t) -> p g t", g=2), [("ZERO",)]
                            elif seg['kind'] == 'p':
                                srcap, rk = HISTF[:, l, :, f, :], [("HISTF",)]
                            else:
                                srcap, rk = SFT[:, l].rearrange("p t (g f) -> p g t f", g=2)[:, :, :, f], [("SFT", l)]
                            act_copy(U[:, :, col:col + 2], srcap, reads=rk + (dk if col > 0 else []), writes=[("UGVH", ug)])
                            tail = U[:, :, col + n:col + n + 2]
                            if seg['kind'] == 'p' and not seg['last']:
                                act_copy(HISTF[:, l, :, f, :], tail, reads=dk, writes=[("HISTF",)])
                            if seg['last']:
                                act_copy(STF[:, si].rearrange("p t (g f) -> p g t f", g=2)[:, :, :, f], tail,
                                         reads=dk, writes=[("STF", si)])
                        for k in (1, 0):
                            for gv in range(2):
                                P.op('dve', lambda v, gv=gv, U=U, t_=tg[gv], f=f, k=k: v.scalar_tensor_tensor(
                                    out=TMPFb(t_, Tt), in0=U[:, gv, k:k + Tt], scalar=pb(328 + k * 88 + gv * CF + f),
                                    in1=TMPFb(t_, Tt), op0=ALU.mult, op1=ALU.add),
                                    reads=[("UGV", ug, gv), ("UGVH", ug), ("TMPF", tg[gv]), ("PB",)], writes=[("TMPF", tg[gv])])
                        sl_ = nxt('tmpf', NTMP)
                        P.op('act', lambda a, sl_=sl_, t_=tg[0]: a.activation(out=TMPFb(sl_, Tt), in_=TMPFb(t_, Tt), func=AF.Silu),
                             reads=[("TMPF", tg[0])], writes=[("TMPF", sl_)])
                        P.op('dve', lambda v, sl_=sl_, t_=tg[1], bq=bq, i=i: v.tensor_tensor(
                            out=ACTQc(bq, i, 0, Tt), in0=TMPFb(sl_, Tt), in1=TMPFb(t_, Tt), op=ALU.mult),
                            reads=[("TMPF", sl_), ("TMPF", tg[1])], writes=[("ACTQ", bq, i)])

                def down_quarter(q, with_stats=False):
                    bq = q % 2
                    wd_v = w_down[l][q * 1408:(q + 1) * 1408, :].rearrange("(kc p) n -> p kc n", p=128)
                    AK = [("ACTQ", bq, i) for i in range(11)]
                    st3 = rms_begin() if with_stats else None
                    pend = []
                    for j in range(DC):
                        wi = wload(wd_v[:, :, j * 128:(j + 1) * 128], 11)
                        s_ = mm_job(11, lambda k, wi=wi: W[:, wi, k, :],
                                    lambda k, n: ACTQc(bq, k, n * Th, (n + 1) * Th), AK + [("W", wi)], Th)
                        P.op('dve', lambda v, s_=s_, j=j: v.tensor_tensor(
                            out=v3(X[:, j, 0:Tt]), in0=psv(s_, Th), in1=v3(X[:, j, 0:Tt]), op=ALU.add),
                            reads=[("ps", s_), ("X", j)], writes=[("X", j)])
                        if with_stats:
                            pend.append(j)
                            if len(pend) > 2:
                                rms_chunk(st3, pend.pop(0), Tt, Th)
                    if with_stats:
                        for j in pend:
                            rms_chunk(st3, j, Tt, Th)
                        rms_end(st3, Tt, Th, AF.Sigmoid if l < n_layers - 1 else None)
                        nready['v'] = True

                up_quarter(0)
                up_quarter(1)
                down_quarter(0)
                up_quarter(2)
                down_quarter(1)
                up_quarter(3)
                if last_tile:
                    for si, seg in enumerate(segs):
                        nf_dst = nfp if seg['kind'] == 'p' else nfs
                        s_ = nxt('ps', NPS)

                        def emit(t, s_=s_, si=si):
                            for tt in range(2):
                                ins = t.transpose(PS[0:88, 2 * s_, tt * 128:(tt + 1) * 128], STF[:, si, tt, :], IDENT[:, :])
                            return ins
                        P.op('pe', emit, reads=[("STF", si), ("IDENT",)], writes=[("ps", s_)])
                        act_copy(OUTF[:, si].rearrange("c t n -> c (t n)"), PS[0:88, 2 * s_, 0:256],
                                 reads=[("ps", s_)], writes=[("OUTF", si)])
                        P.op('sp', lambda s, si=si, nf_dst=nf_dst: s.dma_start(
                            out=nf_dst[l].rearrange("t (c n) -> c t n", n=128), in_=OUTF[:, si]),
                            reads=[("OUTF", si)], dma="OF%d" % si, is_output=True)
                down_quarter(2)
                down_quarter(3, with_stats=True)

            for l in range(n_layers):
                do_layer(l)

            if not nready['v']:
                rms_stats(Tt, Th)
            nready['v'] = False
            rms_apply(Tt, lambda c: PBF[:, c:c + 1], lambda c: X[:, c, 0:Tt], lambda c: ("X", c))
            for seg in segs:
                dst = yp if seg['kind'] == 'p' else ys
                a = 0
                while a < seg['n']:
                    nb = min(128, seg['n'] - a)
                    tok = seg['tok0'] + a
                    col = seg['col'] + a
                    yo = nxt('yout', 2)
                    for half in range(2):
                        s_ = nxt('ps', NPS)

                        def emit(t, s_=s_, nb=nb, half=half, col=col):
                            for j in range(8):
                                cc = half * 8 + j
                                ins = t.transpose(PS[0:nb, 2 * s_ + j // 4, (j % 4) * 128:(j % 4 + 1) * 128],
                                                  X[:, cc, col:col + nb], IDENT[:, :])
                            return ins
                        P.op('pe', emit, reads=XK + [("IDENT",)], writes=[("ps", s_)])
                        copy_alt(YOUTb(yo)[0:nb, half * 1024:(half + 1) * 1024].rearrange("p (a n) -> p a n", a=2),
                                 PS[0:nb, 2 * s_:2 * s_ + 2, :], reads=[("ps", s_)], writes=[("YOUT", yo, half)])
                    P.op('sp', lambda s, yo=yo, nb=nb, tok=tok, dst=dst:
                         s.dma_start(out=dst[tok:tok + nb, :], in_=YOUTb(yo)[0:nb, :]),
                         reads=[("YOUT", yo, 0), ("YOUT", yo, 1)], dma="YO%d" % yo, is_output=True)
                    a += nb

        for ti, tile in enumerate(TILES):
            if ti == len(TILES) - 1:
                load_states()
            do_tile(ti, tile)


    P0 = Prog()
    slabs = []
    emit_all(P0, slabs, True)
    P = Prog()
    emit_all(P, slabs, False)
    P.finalize()

    esem = {e: es.enter_context(nc.semaphore("sem_" + e)) for e in ENGS}
    dsem = {n: es.enter_context(nc.semaphore("dsem_" + n)) for n in sorted(P.dcnt)}
    with nc.Block() as block:
        @block.tensor
        def _(t):
            P.replay('pe', t, esem, dsem)

        @block.scalar
        def _(a):
            P.replay('act', a, esem, dsem)

        @block.vector
        def _(v):
            P.replay('dve', v, esem, dsem)

        @block.gpsimd
        def _(g):
            P.replay('pool', g, esem, dsem)

        @block.sync
        def _(s):
            P.replay('sp', s, esem, dsem)
    es.close()
    return nc


def make_in_maps(inp, n_cores=8):
    f = lambda a: np.ascontiguousarray(np.asarray(a, dtype=np.float32))
    shared = dict(
        n1g=f(inp["norm1_g"]), w_in=f(inp["w_in"]), caw=f(inp["conv_a_w"]), cab=f(inp["conv_a_b"]),
        lng=f(inp["ln_a_g"]), lnb=f(inp["ln_a_b"]), cbw=f(inp["conv_b_w"]), w_out=f(inp["w_out"]),
        n2g=f(inp["norm2_g"]), w_up=f(inp["w_up"]), cfw=f(inp["conv_ffn_w"]), w_down=f(inp["w_down"]),
        fg=f(inp["final_g"]))
    xp = f(inp["x_prompt"]); xs = f(inp["x_sample"])
    sa = f(inp["state_conv_a"]); sb = f(inp["state_conv_b"]); sf = f(inp["state_ffn"])
    maps = []
    for b in range(n_cores):
        m = dict(shared)
        m.update(xp=f(xp[b]), xs=f(xs[b]), sa=f(sa[:, b]), sb=f(sb[:, b]), sf=f(sf[:, b]))
        maps.append(m)
    return maps


def kernel(**inputs):
    nc = build_program()
    maps = make_in_maps(inputs)
    res = run_bass_kernel_spmd(nc, maps, core_ids=list(range(8)))
    R = res.results
    st = lambda name, ax: np.stack([np.asarray(R[b][name], dtype=np.float32) for b in range(8)], axis=ax)
    return (st("yp", 0), st("ys", 0), st("nap", 1), st("nbp", 1), st("nfp", 1),
            st("nas", 1), st("nbs", 1), st("nfs", 1))
```

```python
import numpy as np
from contextlib import ExitStack
import concourse.bass as bass
import concourse.mybir as mybir
from concourse.bass_utils import run_bass_kernel_spmd

F32 = mybir.dt.float32
BF16 = mybir.dt.bfloat16
AF = mybir.ActivationFunctionType
ALU = mybir.AluOpType

L = 4
D = 2048
DC = 16
CA = 8
CF = 44
DFF = 5632
HA = 30
KA = 31
TMAX = 704
EPS = 1e-6
NW = 6
NPS = 4
NTMP = 6
NPT = 0
PBW = 640

TILES = [
    dict(T=704, segs=[dict(kind='p', tok0=0, n=704, col=0, first=True, last=False)]),
    dict(T=704, segs=[dict(kind='p', tok0=704, n=704, col=0, first=False, last=False)]),
    dict(T=704, segs=[dict(kind='p', tok0=1408, n=640, col=0, first=False, last=True),
                      dict(kind='s', tok0=0, n=32, col=670, first=False, last=True)],
         gaps=[(640, 670), (702, 704)]),
]

ENGS = ['pe', 'act', 'dve', 'pool', 'sp']


class Prog:
    def __init__(self):
        self.q = {e: [] for e in ENGS}
        self.cnt = {e: 0 for e in ENGS}
        self.dcnt = {}
        self.lastw = {}
        self.readers = {}
        self.kview = {}
        self.rview = {}
        self.rkeys = {}
        self.rfence = {}
        self.out_tokens = []

    @staticmethod
    def _compress(toks):
        m = {}
        for t in toks:
            k = (t[0], t[1])
            if m.get(k, 0) < t[2]:
                m[k] = t[2]
        return [(a, b, n) for (a, b), n in m.items()]

    def _touch(self, k, waits):
        rv = self.kview.get(k[0])
        if rv is None:
            return
        reg, view = rv
        if self.rview.get(reg) != view:
            fence = []
            for kk in self.rkeys.get(reg, ()):
                if kk in self.lastw:
                    fence.append(self.lastw.pop(kk))
                fence.extend(self.readers.pop(kk, []))
            self.rfence[reg] = self._compress(fence)
            self.rview[reg] = view
            self.rkeys[reg] = set()
        self.rkeys[reg].add(k)
        waits.extend(self.rfence.get(reg, []))

    def op(self, eng, emit, reads=(), writes=(), dma=None, is_output=False):
        raw = []
        other = []
        for k in reads:
            self._touch(k, other)
            w = self.lastw.get(k)
            if w is not None:
                raw.append(w)
        for k in writes:
            self._touch(k, other)
            w = self.lastw.get(k)
            if w is not None:
                other.append(w)
            other.extend(self.readers.get(k, []))
        waits = list(raw)
        for t in other:
            if t[0] == 'e' and t[1] == eng:
                continue
            waits.append(t)
        if eng == 'pe':
            waits = [t for t in waits if not (t[0] == 'e' and t[1] == 'pe')]
        waits = self._compress(waits)
        if dma is None:
            self.cnt[eng] += 1
            tok = ('e', eng, self.cnt[eng])
        else:
            self.dcnt[dma] = self.dcnt.get(dma, 0) + 1
            tok = ('d', dma, self.dcnt[dma])
        for k in reads:
            self.readers.setdefault(k, []).append(tok)
        for k in writes:
            self.lastw[k] = tok
            self.readers[k] = []
        self.q[eng].append((waits, emit, tok))
        if is_output:
            self.out_tokens.append(tok)
        return tok

    def finalize(self):
        waited = {e: set() for e in ENGS}
        for e in ENGS:
            for waits, emit, tok in self.q[e]:
                for t in waits:
                    if t[0] == 'e':
                        waited[t[1]].add(t[2])
        for t in self.out_tokens:
            if t[0] == 'e':
                waited[t[1]].add(t[2])
        self.newval = {e: {n: i + 1 for i, n in enumerate(sorted(waited[e]))} for e in ENGS}

    def replay(self, eng, engine, esem, dsem):
        seen = {}

        def do_wait(t):
            if t[0] == 'e':
                sem, val = esem[t[1]], self.newval[t[1]][t[2]]
            else:
                sem, val = dsem[t[1]], 16 * t[2]
            key = (t[0], t[1])
            if seen.get(key, 0) < val:
                engine.wait_ge(sem, val)
                seen[key] = val

        for waits, emit, tok in self.q[eng]:
            for t in waits:
                do_wait(t)
            ins = emit(engine)
            if tok[0] == 'e':
                if tok[2] in self.newval[eng]:
                    ins.then_inc(esem[eng], 1)
            else:
                ins.then_inc(dsem[tok[1]], 16)
        if eng == 'sp':
            for t in self._compress(self.out_tokens):
                do_wait(t)


def build_program(n_layers=L):
    nc = bass.Bass("TRN2", target_bir_lowering=False)

    def din(name, shape):
        return nc.dram_tensor(name, shape, F32, kind="ExternalInput").ap()

    def dout(name, shape):
        return nc.dram_tensor(name, shape, F32, kind="ExternalOutput").ap()

    xp = din("xp", [2048, D]); xs = din("xs", [32, D])
    sa = din("sa", [L, 30, 1024]); sbd = din("sb", [L, 2, 1024]); sfd = din("sf", [L, 2, 2 * DFF])
    n1g = din("n1g", [L, D]); w_in = din("w_in", [L, D, 5120]); caw = din("caw", [L, KA, 1024])
    cab = din("cab", [L, 1024]); lng = din("lng", [L, 1024]); lnb = din("lnb", [L, 1024])
    cbw = din("cbw", [L, 3, 1024]); w_out = din("w_out", [L, D, D]); n2g = din("n2g", [L, D])
    w_up = din("w_up", [L, D, 2 * DFF]); cfw = din("cfw", [L, 3, 2 * DFF]); w_down = din("w_down", [L, DFF, D])
    fg = din("fg", [D])
    yp = dout("yp", [2048, D]); ys = dout("ys", [32, D])
    nap = dout("nap", [L, 30, 1024]); nbp = dout("nbp", [L, 2, 1024]); nfp = dout("nfp", [L, 2, 2 * DFF])
    nas = dout("nas", [L, 30, 1024]); nbs = dout("nbs", [L, 2, 1024]); nfs = dout("nfs", [L, 2, 2 * DFF])

    es = ExitStack()

    def sb(name, shape, dt=F32):
        return es.enter_context(nc.sbuf_tensor(name, shape, dt))

    X = sb("X", [128, DC, TMAX])
    H = sb("H", [128, DC, TMAX], BF16)
    PW = 8448 + NTMP * TMAX + 2 * 734 + 2 * 706
    PR = sb("PR", [128, PW])
    SQ = sb("SQ", [128, 3, TMAX], BF16)
    SQRT = sb("SQRT", [128, TMAX]); RSTD = sb("RSTD", [128, TMAX])
    MU = sb("MU", [128, TMAX]); VAR = sb("VAR", [128, TMAX])
    W = sb("W", [128, NW, 16, 128], BF16)
    PB = sb("PB", [128, L, PBW]); PBF = sb("PBF", [128, 16]); PSTF = sb("PSTF", [16, 128])
    HISTA = sb("HISTA", [128, L, CA, HA]); HISTB = sb("HISTB", [128, L, CA, 2])
    HISTF = sb("HISTF", [128, L, 2, CF, 2])
    SAT = sb("SAT", [128, L, 2, HA, 4]); SBT = sb("SBT", [128, L, 2, CA]); SFT = sb("SFT", [128, L, 2, 2 * CF])
    SA_in = sb("SA_in", [120, 256]); SB_in = sb("SB_in", [16, 128]); SF_in = sb("SF_in", [88, 2, 128])
    STA = sb("STA", [128, 2, 2, HA, 4]); STB = sb("STB", [128, 2, 2, CA]); STF = sb("STF", [128, 2, 2, 2 * CF])
    OUTA = sb("OUTA", [120, 2, 256]); OUTB = sb("OUTB", [16, 2, 128]); OUTF = sb("OUTF", [88, 2, 2, 128])
    IDENT = sb("IDENT", [128, 128]); ONESB = sb("ONESB", [128, 128], BF16); ONESB2 = sb("ONESB2", [128, 128], BF16)
    PST = sb("PST", [128, 5, 128])
    ZERO = sb("ZERO", [128, 32]); EPSC = sb("EPSC", [128, 1]); DUMMY = sb("DUMMY", [128, 4])
    PS = es.enter_context(nc.psum_tensor("PS", [128, 2 * NPS, 512], F32))

    def emit_all(P, slabs, dry):
        for k in ("R", "MB", "TMP", "AGLU", "AGLUH", "CH", "CHH"):
            P.kview[k] = ("PR", "mix")
        for k in ("ACTQ", "UGV", "UGVH", "TMPF"):
            P.kview[k] = ("PR", "ffn")
        P.kview["XIN"] = ("PR", "io_in")
        P.kview["YOUT"] = ("PR", "io_out")

        R_bf = PR[:, 0:5632].bitcast(BF16)
        MB_bf = PR[:, 5632:8448].bitcast(BF16)
        ACTQ_bf = PR[:, 0:7744].bitcast(BF16)

        def ACONV(c, Tt): return PR[:, c * TMAX: c * TMAX + Tt]
        def MAc(c, a, b): return R_bf[:, c * TMAX + a: c * TMAX + b]
        def MBc(c, a, b): return MB_bf[:, c * TMAX + a: c * TMAX + b]
        def TMPb(i, Tt): return PR[:, 8448 + i * TMAX: 8448 + i * TMAX + Tt]
        def TMPbf(i, a, b): return PR[:, 8448 + i * TMAX: 8448 + (i + 1) * TMAX].bitcast(BF16)[:, a:b]
        AG0 = 8448 + NTMP * TMAX
        def AGLUb(i, a, b): return PR[:, AG0 + i * 734 + a: AG0 + i * 734 + b]
        CH0 = AG0 + 2 * 734
        def CHb(i, a, b): return PR[:, CH0 + i * 706 + a: CH0 + i * 706 + b]
        def ACTQc(bq, i, a, b): return ACTQ_bf[:, (bq * 11 + i) * TMAX + a: (bq * 11 + i) * TMAX + b]
        def UGVb(i): return PR[:, 7744 + i * 1412: 7744 + (i + 1) * 1412].rearrange("p (g t) -> p g t", g=2)
        def TMPFb(i, Tt): return PR[:, 10568 + i * TMAX: 10568 + i * TMAX + Tt]
        def XINb(i): return PR[:, i * 2048:(i + 1) * 2048]
        def YOUTb(i): return PR[:, i * 2048:(i + 1) * 2048]

        def v3(ap):
            return ap.rearrange("p (a n) -> p a n", a=2)

        ctr = dict(ps=0, w=0, tmp=0, sq=0, aglu=0, ch=0, ugv=0, tmpf=0, xin=0, yout=0, alt=0, alt2=0)

        ps_reserved = set()

        def nxt(name, n):
            while True:
                v = ctr[name]
                ctr[name] = (v + 1) % n
                if name != 'ps' or v not in ps_reserved:
                    return v

        def psv(s, Th):
            return PS[:, 2 * s:2 * s + 2, 0:Th]

        P.op('pool', lambda g: g.memset(IDENT[:], 0.0), writes=[("IDENT",)])
        P.op('pool', lambda g: g.affine_select(out=IDENT[:], in_=IDENT[:], pattern=[[-1, 128]],
                                               compare_op=ALU.not_equal, fill=1.0, base=0, channel_multiplier=1),
             reads=[("IDENT",)], writes=[("IDENT",)])
        P.op('dve', lambda v: v.memset(ONESB[:], 1.0 / D), writes=[("ONESB",)])
        P.op('dve', lambda v: v.memset(ONESB2[:], 1.0 / 1024), writes=[("ONESB2",)])
        P.op('dve', lambda v: v.memset(ZERO[:], 0.0), writes=[("ZERO",)])
        P.op('dve', lambda v: v.memset(EPSC[:], EPS), writes=[("EPSC",)])

        def copy_alt(out, in_, reads, writes):
            if nxt('alt', 2) == 0:
                P.op('act', lambda a: a.activation(out=out, in_=in_, func=AF.Copy), reads=reads, writes=writes)
            else:
                P.op('dve', lambda v: v.tensor_copy(out=out, in_=in_), reads=reads, writes=writes)

        def act_copy(out, in_, reads, writes):
            P.op('act', lambda a: a.activation(out=out, in_=in_, func=AF.Copy), reads=reads, writes=writes)

        def load_params():
            def rows2(ap1d_or_2d, pat, **kw):
                return ap1d_or_2d.rearrange(pat, **kw)

            for l in range(n_layers):
                pieces = [
                    (rows2(n1g[l], "(c n) -> c n", n=128), 0, 16),
                    (rows2(n2g[l], "(c n) -> c n", n=128), 16, 16),
                    (rows2(cab[l], "(c n) -> c n", n=128), 32, 8),
                    (rows2(lng[l], "(c n) -> c n", n=128), 40, 8),
                    (rows2(lnb[l], "(c n) -> c n", n=128), 48, 8),
                    (rows2(cbw[l], "k (c n) -> (k c) n", n=128), 56, 24),
                    (rows2(caw[l], "k (c n) -> (k c) n", n=128), 80, 248),
                    (rows2(cfw[l], "k (c n) -> (k c) n", n=128), 328, 264),
                ]
                for src, r0, nr in pieces:
                    a = 0
                    while a < nr:
                        g = (r0 + a) // 128
                        p0 = (r0 + a) % 128
                        m = min(nr - a, 128 - p0)
                        P.op('act' if nxt('alt2', 2) == 0 else 'sp', lambda s, src=src, a=a, m=m, g=g, p0=p0:
                             s.dma_start(out=PST[p0:p0 + m, g, :], in_=src[a:a + m, :]),
                             writes=[("PST", g)], dma="par")
                        a += m
                s_ = nxt('ps', NPS)

                def emit_pt(t, s_=s_):
                    for g in range(5):
                        rows = min(128, 592 - g * 128)
                        ins = t.transpose(PS[:, 2 * s_ + g // 4, (g % 4) * 128:(g % 4) * 128 + rows],
                                          PST[0:rows, g, :], IDENT[0:rows, 0:rows])
                    return ins
                P.op('pe', emit_pt, reads=[("PST", g) for g in range(5)] + [("IDENT",)], writes=[("ps", s_)])
                P.op('dve', lambda v, s_=s_, l=l: v.tensor_copy(out=PB[:, l, 0:512], in_=PS[:, 2 * s_, 0:512]),
                     reads=[("ps", s_)], writes=[("PB",)])
                P.op('dve', lambda v, s_=s_, l=l: v.tensor_copy(out=PB[:, l, 512:592], in_=PS[:, 2 * s_ + 1, 0:80]),
                     reads=[("ps", s_)], writes=[("PB",)])
            P.op('sp', lambda s: s.dma_start(out=PSTF[:, :], in_=fg.rearrange("(c n) -> c n", n=128)),
                 writes=[("PSTF",)], dma="parf")
            s_ = nxt('ps', NPS)
            P.op('pe', lambda t, s_=s_: t.transpose(PS[:, 2 * s_, 0:16], PSTF[0:16, :], IDENT[0:16, 0:16]),
                 reads=[("PSTF",), ("IDENT",)], writes=[("ps", s_)])
            P.op('dve', lambda v, s_=s_: v.tensor_copy(out=PBF[:, :], in_=PS[:, 2 * s_, 0:16]),
                 reads=[("ps", s_)], writes=[("PBF",)])


        def load_states():
            for l in range(n_layers):
                P.op('sp', lambda s, l=l: s.dma_start(out=SA_in[:, :], in_=sa[l].rearrange("t (q n) -> (t q) n", n=256)),
                     writes=[("SA_in",)], dma="sa_in")
                s_ = nxt('ps', NPS)

                def emit_sa(t, s_=s_):
                    for j in range(2):
                        ins = t.transpose(PS[:, 2 * s_, j * 128:j * 128 + 120], SA_in[0:120, j * 128:(j + 1) * 128],
                                          IDENT[0:120, 0:120])
                    return ins
                P.op('pe', emit_sa, reads=[("SA_in",), ("IDENT",)], writes=[("ps", s_)])
                P.op('act', lambda a, s_=s_, l=l: a.activation(
                    out=SAT[:, l].rearrange("p j t q -> p j (t q)"),
                    in_=PS[:, 2 * s_, 0:256].rearrange("p (j n) -> p j n", j=2)[:, :, 0:120], func=AF.Copy),
                    reads=[("ps", s_)], writes=[("SAT", l)])
                P.op('sp', lambda s, l=l: s.dma_start(out=SB_in[:, :], in_=sbd[l].rearrange("t (c n) -> (t c) n", n=128)),
                     writes=[("SB_in",)], dma="sb_in")
                s_ = nxt('ps', NPS)
                P.op('pe', lambda t, s_=s_: t.transpose(PS[:, 2 * s_, 0:16], SB_in[0:16, :], IDENT[0:16, 0:16]),
                     reads=[("SB_in",), ("IDENT",)], writes=[("ps", s_)])
                P.op('act', lambda a, s_=s_, l=l: a.activation(
                    out=SBT[:, l].rearrange("p t c -> p (t c)"), in_=PS[:, 2 * s_, 0:16], func=AF.Copy),
                    reads=[("ps", s_)], writes=[("SBT", l)])
                P.op('sp', lambda s, l=l: s.dma_start(out=SF_in[:, :, :], in_=sfd[l].rearrange("t (c n) -> c t n", n=128)),
                     writes=[("SF_in",)], dma="sf_in")
                s_ = nxt('ps', NPS)

                def emit_sf(t, s_=s_):
                    for tt in range(2):
                        ins = t.transpose(PS[:, 2 * s_, tt * 128:tt * 128 + 88], SF_in[0:88, tt, :], IDENT[0:88, 0:88])
                    return ins
                P.op('pe', emit_sf, reads=[("SF_in",), ("IDENT",)], writes=[("ps", s_)])
                P.op('act', lambda a, s_=s_, l=l: a.activation(
                    out=SFT[:, l], in_=PS[:, 2 * s_, 0:256].rearrange("p (t n) -> p t n", t=2)[:, :, 0:88], func=AF.Copy),
                    reads=[("ps", s_)], writes=[("SFT", l)])


        wst = dict(n=0, issued=0)

        def wload(src, nk):
            n = wst['n']
            wst['n'] += 1
            if dry:
                slabs.append((src, nk))
                return n % NW
            upto = min(n + NW - 2, len(slabs) - 1)
            while wst['issued'] <= upto:
                m = wst['issued']
                s_src, s_nk = slabs[m]
                i = m % NW
                P.op('pool', lambda g, i=i, s_src=s_src, s_nk=s_nk: g.dma_start(out=W[:, i, 0:s_nk, :], in_=s_src),
                     writes=[("W", i)], dma="W%d" % i)
                wst['issued'] += 1
            return n % NW

        def mm_job(nk, lhsT_fn, rhs_fn, reads, Th, fine=None):
            s = nxt('ps', NPS)
            if fine is not None:
                for k in range(nk):
                    def emit_k(t, k=k):
                        lt = lhsT_fn(k)
                        for n in range(2):
                            ins = t.matmul(PS[:, 2 * s + n, 0:Th], lhsT=lt, rhs=rhs_fn(k, n),
                                           start=(k == 0), stop=(k == nk - 1))
                        return ins
                    P.op('pe', emit_k, reads=[fine(k)] + reads, writes=[("ps", s)])
                return s

            def emit(t):
                ins = None
                for k in range(nk):
                    lt = lhsT_fn(k)
                    for n in range(2):
                        ins = t.matmul(PS[:, 2 * s + n, 0:Th], lhsT=lt, rhs=rhs_fn(k, n),
                                       start=(k == 0), stop=(k == nk - 1))
                return ins
            P.op('pe', emit, reads=reads, writes=[("ps", s)])
            return s

        XK = [("X", c) for c in range(DC)]
        HK = [("H", c) for c in range(DC)]

        def act_preload(func, col):
            P.op('act', lambda a: a.activation(out=DUMMY[:, col:col + 1], in_=EPSC[:, 0:1], func=func),
                 reads=[("EPSC",)], writes=[("DUMMY", col)])

        def rms_begin():
            s = nxt('ps', NPS)
            ps_reserved.add(s)
            return dict(s=s)

        def rms_chunk(st, c, Tt, Th):
            s = st['s']
            i = nxt('sq', 3)
            P.op('act', lambda a, i=i, c=c: a.activation(out=SQ[:, i, 0:Tt], in_=X[:, c, 0:Tt], func=AF.Square),
                 reads=[("X", c)], writes=[("SQ", i)])

            def emit(t, i=i, c=c):
                for n in range(2):
                    ins = t.matmul(PS[:, 2 * s + n, 0:Th], lhsT=ONESB[:, :], rhs=SQ[:, i, n * Th:(n + 1) * Th],
                                   start=(c == 0), stop=(c == DC - 1))
                return ins
            P.op('pe', emit, reads=[("SQ", i), ("ONESB",)], writes=[("ps", s)])

        def rms_end(st, Tt, Th, next_func=None):
            s = st['s']
            act_preload(AF.Sqrt, 0)
            P.op('act', lambda a: a.activation(out=v3(SQRT[:, 0:Tt]), in_=psv(s, Th), func=AF.Sqrt, bias=EPSC[:, 0:1]),
                 reads=[("ps", s), ("EPSC",)], writes=[("SQRT",)])
            ps_reserved.discard(s)
            if next_func is not None:
                act_preload(next_func, 1)
            P.op('dve', lambda v: v.reciprocal(out=RSTD[:, 0:Tt], in_=SQRT[:, 0:Tt]),
                 reads=[("SQRT",)], writes=[("RSTD",)])

        def rms_stats(Tt, Th, next_func=None):
            st = rms_begin()
            for c in range(DC):
                rms_chunk(st, c, Tt, Th)
            rms_end(st, Tt, Th, next_func)

        def rms_apply(Tt, gcol, out_fn, out_key):
            for c in range(DC):
                P.op('dve', lambda v, c=c: v.scalar_tensor_tensor(
                    out=out_fn(c), in0=X[:, c, 0:Tt], scalar=gcol(c), in1=RSTD[:, 0:Tt], op0=ALU.mult, op1=ALU.mult),
                    reads=[("X", c), ("RSTD",), ("PB",), ("PBF",)], writes=[out_key(c)])

        def do_tile(ti, tile):
            Tt = tile['T']
            Th = Tt // 2
            segs = tile['segs']

            for g0, g1 in tile.get('gaps', []):
                P.op('dve', lambda v, g0=g0, g1=g1: v.memset(X[:, :, g0:g1], 0.0), writes=XK)
            for seg in segs:
                src = xp if seg['kind'] == 'p' else xs
                a = 0
                while a < seg['n']:
                    nb = min(128, seg['n'] - a)
                    tok = seg['tok0'] + a
                    col = seg['col'] + a
                    xi = nxt('xin', 2)
                    P.op('sp', lambda s, xi=xi, nb=nb, tok=tok, src=src:
                         s.dma_start(out=XINb(xi)[0:nb, :], in_=src[tok:tok + nb, :]),
                         writes=[("XIN", xi)], dma="XIN%d" % xi)
                    for half in range(2):
                        s_ = nxt('ps', NPS)

                        def emit(t, s_=s_, xi=xi, nb=nb, half=half):
                            for j in range(8):
                                cc = half * 8 + j
                                ins = t.transpose(PS[:, 2 * s_ + j // 4, (j % 4) * 128:(j % 4) * 128 + nb],
                                                  XINb(xi)[0:nb, cc * 128:(cc + 1) * 128], IDENT[0:nb, 0:nb])
                            return ins
                        P.op('pe', emit, reads=[("XIN", xi), ("IDENT",)], writes=[("ps", s_)])
                        copy_alt(X[:, half * 8:half * 8 + 8, col:col + nb],
                                 PS[:, 2 * s_:2 * s_ + 2, :].rearrange("p a (b n) -> p (a b) n", n=128)[:, :, 0:nb],
                                 reads=[("ps", s_)], writes=[("X", half * 8 + j) for j in range(8)])
                    a += nb

            if ti == 0:
                load_params()
            nready = dict(v=False)

            def do_layer(l):
                def pb(col):
                    return PB[:, l, col:col + 1]

                if not nready['v']:
                    rms_stats(Tt, Th, AF.Sigmoid)
                nready['v'] = False
                rms_apply(Tt, lambda c: pb(c), lambda c: H[:, c, 0:Tt], lambda c: ("H", c))

                w_in_v = w_in[l].rearrange("(kc p) n -> p kc n", p=128)

                def inproj(group, c):
                    col0 = group * 1024 + c * 128
                    wi = wload(w_in_v[:, :, col0:col0 + 128], 16)
                    if False and c == 0 and group in (0, 1):
                        return mm_job(16, lambda k, wi=wi: W[:, wi, k, :],
                                      lambda k, n: H[:, k, n * Th:(n + 1) * Th], [("W", wi)], Th,
                                      fine=lambda k: ("H", k))
                    return mm_job(16, lambda k, wi=wi: W[:, wi, k, :],
                                  lambda k, n: H[:, k, n * Th:(n + 1) * Th], HK + [("W", wi)], Th)

                def inproj_pair_split(ga, gb, c):
                    specs = []
                    for g_ in (ga, gb):
                        col0 = g_ * 1024 + c * 128
                        wi = wload(w_in_v[:, :, col0:col0 + 128], 16)
                        specs.append((wi, nxt('ps', NPS)))
                    for k0, k1 in ((0, 8), (8, 16)):
                        for wi, s_ in specs:
                            def emit_h(t, wi=wi, s_=s_, k0=k0, k1=k1):
                                for k in range(k0, k1):
                                    for n in range(2):
                                        ins = t.matmul(PS[:, 2 * s_ + n, 0:Th], lhsT=W[:, wi, k, :],
                                                       rhs=H[:, k, n * Th:(n + 1) * Th], start=(k == 0), stop=(k == 15))
                                return ins
                            P.op('pe', emit_h, reads=[("H", k) for k in range(k0, k1)] + [("W", wi)], writes=[("ps", s_)])
                    return specs[0][1], specs[1][1]

                def hist_ops(seg_list, H_, bufslice, hkey, dkey, zero_src, hist_src, state_src, hist_dst, st_dst, l=l):
                    for si, seg in enumerate(seg_list):
                        col, n = seg['col'], seg['n']
                        if seg['kind'] == 'p' and seg['first']:
                            srcap, rk = zero_src, [("ZERO",)]
                        elif seg['kind'] == 'p':
                            srcap, rk = hist_src, [("HIST",)]
                        else:
                            srcap, rk = state_src, [("SAT", l), ("SBT", l), ("SFT", l)]
                        act_copy(bufslice(col, col + H_), srcap, reads=rk + ([dkey] if col > 0 else []), writes=[hkey])
                        tail = bufslice(col + n, col + n + H_)
                        if seg['kind'] == 'p' and not seg['last']:
                            act_copy(hist_dst, tail, reads=[dkey], writes=[("HIST",)])
                        if seg['last']:
                            act_copy(st_dst(si), tail, reads=[dkey], writes=[("ST", si)])

                last_tile = segs[0]['last']

                acc1_slot = [None]

                def doA(c, mid_cb=None):
                    if c == 0:
                        sv, sg = inproj_pair_split(0, 1, c)
                    else:
                        sv = inproj(0, c)
                        sg = inproj(1, c)
                    ts_ = nxt('tmp', NTMP)
                    P.op('act', lambda a, ts_=ts_, sg=sg: a.activation(out=v3(TMPb(ts_, Tt)), in_=psv(sg, Th), func=AF.Sigmoid),
                         reads=[("ps", sg)], writes=[("TMP", ts_)])
                    ag = nxt('aglu', 2)
                    P.op('dve', lambda v, ag=ag, sv=sv, ts_=ts_: v.tensor_tensor(
                        out=v3(AGLUb(ag, HA, HA + Tt)), in0=psv(sv, Th), in1=v3(TMPb(ts_, Tt)), op=ALU.mult),
                        reads=[("ps", sv), ("TMP", ts_)], writes=[("AGLU", ag)])
                    j_, q_ = c % 2, c // 2
                    hist_ops(segs, HA, lambda a, b, ag=ag: AGLUb(ag, a, b), ("AGLUH", ag), ("AGLU", ag),
                             ZERO[:, 0:HA], HISTA[:, l, c, :], SAT[:, l, j_, :, q_], HISTA[:, l, c, :],
                             lambda si, j_=j_, q_=q_: STA[:, si, j_, :, q_])
                    acc0 = ACONV(c, Tt)
                    t1 = nxt('tmp', NTMP)
                    acc1_slot[0] = t1
                    acc1 = TMPb(t1, Tt)
                    k0 = [("R", 2 * c), ("R", 2 * c + 1)]
                    k1 = [("TMP", t1)]
                    src_keys = [("AGLU", ag), ("AGLUH", ag), ("PB",)]
                    P.op('act', lambda a, ag=ag, acc0=acc0, c=c: a.activation(
                        out=acc0, in_=AGLUb(ag, 0, Tt), func=AF.Identity, scale=pb(80 + c), bias=pb(32 + c)),
                        reads=src_keys, writes=k0)
                    P.op('act', lambda a, ag=ag, acc1=acc1, c=c: a.activation(
                        out=acc1, in_=AGLUb(ag, 1, 1 + Tt), func=AF.Copy, scale=pb(80 + 8 + c)),
                        reads=src_keys, writes=k1)
                    if NPT > 0:
                        a2 = nxt('tmp', NTMP)
                        t2 = nxt('tmp', NTMP)
                        acc2 = TMPb(a2, Tt)
                        tmp2 = TMPb(t2, Tt)
                        for k in range(KA - NPT, KA):
                            if k == KA - NPT:
                                P.op('pool', lambda g, ag=ag, acc2=acc2, c=c, k=k: g.tensor_scalar(
                                    out=acc2, in0=AGLUb(ag, k, k + Tt), scalar1=pb(80 + k * 8 + c), scalar2=None, op0=ALU.mult),
                                    reads=src_keys, writes=[("TMP", a2)])
                            else:
                                P.op('pool', lambda g, ag=ag, tmp2=tmp2, c=c, k=k: g.tensor_scalar(
                                    out=tmp2, in0=AGLUb(ag, k, k + Tt), scalar1=pb(80 + k * 8 + c), scalar2=None, op0=ALU.mult),
                                    reads=src_keys, writes=[("TMP", t2)])
                                P.op('pool', lambda g, acc2=acc2, tmp2=tmp2: g.tensor_tensor(
                                    out=acc2, in0=acc2, in1=tmp2, op=ALU.add),
                                    reads=[("TMP", a2), ("TMP", t2)], writes=[("TMP", a2)])
                    for k in range(2, KA - NPT):
                        if mid_cb is not None and k == 16:
                            mid_cb()
                        acc, kk = (acc0, k0) if k % 2 == 0 else (acc1, k1)
                        P.op('dve', lambda v, ag=ag, acc=acc, c=c, k=k: v.scalar_tensor_tensor(
                            out=acc, in0=AGLUb(ag, k, k + Tt), scalar=pb(80 + k * 8 + c), in1=acc, op0=ALU.mult, op1=ALU.add),
                            reads=src_keys + kk, writes=kk)
                    P.op('dve', lambda v, acc0=acc0, acc1=acc1: v.tensor_tensor(out=acc0, in0=acc0, in1=acc1, op=ALU.add),
                         reads=k0 + k1, writes=k0)
                    if NPT > 0:
                        P.op('dve', lambda v, acc0=acc0, acc2=acc2: v.tensor_tensor(out=acc0, in0=acc0, in1=acc2, op=ALU.add),
                             reads=k0 + [("TMP", a2)], writes=k0)

                def doB(c):
                    sh = inproj(4, c)
                    bh = nxt('tmp', NTMP)
                    P.op('act', lambda a, bh=bh, sh=sh: a.activation(out=v3(TMPb(bh, Tt)), in_=psv(sh, Th), func=AF.Copy),
                         reads=[("ps", sh)], writes=[("TMP", bh)])
                    sc = inproj(3, c)
                    ch = nxt('ch', 2)
                    P.op('dve', lambda v, ch=ch, sc=sc, bh=bh: v.tensor_tensor(
                        out=v3(CHb(ch, 2, 2 + Tt)), in0=psv(sc, Th), in1=v3(TMPb(bh, Tt)), op=ALU.mult),
                        reads=[("ps", sc), ("TMP", bh)], writes=[("CH", ch)])
                    hist_ops(segs, 2, lambda a, b, ch=ch: CHb(ch, a, b), ("CHH", ch), ("CH", ch),
                             ZERO[:, 0:2], HISTB[:, l, c, :], SBT[:, l, :, c], HISTB[:, l, c, :],
                             lambda si, c=c: STB[:, si, :, c])
                    u3 = nxt('tmp', NTMP)
                    srck = [("CH", ch), ("CHH", ch), ("PB",)]
                    P.op('act', lambda a, u3=u3, ch=ch, c=c: a.activation(
                        out=TMPb(u3, Tt), in_=CHb(ch, 0, Tt), func=AF.Copy, scale=pb(56 + c)),
                        reads=srck, writes=[("TMP", u3)])
                    for k in (1, 2):
                        P.op('dve', lambda v, u3=u3, ch=ch, c=c, k=k: v.scalar_tensor_tensor(
                            out=TMPb(u3, Tt), in0=CHb(ch, k, k + Tt), scalar=pb(56 + k * 8 + c), in1=TMPb(u3, Tt),
                            op0=ALU.mult, op1=ALU.add), reads=srck + [("TMP", u3)], writes=[("TMP", u3)])
                    sbb = inproj(2, c)
                    P.op('dve', lambda v, sbb=sbb, u3=u3, c=c: v.tensor_tensor(
                        out=v3(MBc(c, 0, Tt)), in0=psv(sbb, Th), in1=v3(TMPb(u3, Tt)), op=ALU.mult),
                        reads=[("ps", sbb), ("TMP", u3)], writes=[("MB", c)])

                for c in range(CA):
                    if c == CA - 1:
                        doA(c, mid_cb=lambda: doB(CA - 1))
                    else:
                        doA(c)
                        doB(c)

                s_mu = nxt('ps', NPS)
                s_e2 = nxt('ps', NPS)
                for c in range(CA):
                    sq = nxt('tmp', NTMP)
                    if c < CA - 1 and sq == acc1_slot[0]:
                        sq = nxt('tmp', NTMP)
                    kc = [("R", 2 * c), ("R", 2 * c + 1)]
                    P.op('act', lambda a, sq=sq, c=c: a.activation(out=TMPbf(sq, 0, Tt), in_=ACONV(c, Tt), func=AF.Square),
                         reads=kc, writes=[("TMP", sq)])
                    P.op('act', lambda a, sq=sq, c=c: a.activation(out=TMPbf(sq, TMAX, TMAX + Tt), in_=ACONV(c, Tt), func=AF.Copy),
                         reads=kc, writes=[("TMP", sq)])

                    def emit(t, c=c, sq=sq):
                        for n in range(2):
                            t.matmul(PS[:, 2 * s_mu + n, 0:Th], lhsT=ONESB2[:, :], rhs=TMPbf(sq, TMAX + n * Th, TMAX + (n + 1) * Th),
                                     start=(c == 0), stop=(c == CA - 1))
                        for n in range(2):
                            ins = t.matmul(PS[:, 2 * s_e2 + n, 0:Th], lhsT=ONESB2[:, :], rhs=TMPbf(sq, n * Th, (n + 1) * Th),
                                           start=(c == 0), stop=(c == CA - 1))
                        return ins
                    P.op('pe', emit, reads=[("TMP", sq), ("ONESB2",)], writes=[("ps", s_mu), ("ps", s_e2)])
                act_preload(AF.Sqrt, 0)
                P.op('act', lambda a: a.activation(out=v3(MU[:, 0:Tt]), in_=psv(s_mu, Th), func=AF.Copy),
                     reads=[("ps", s_mu)], writes=[("MU",)])
                P.op('dve', lambda v: v.tensor_tensor(out=VAR[:, 0:Tt], in0=MU[:, 0:Tt], in1=MU[:, 0:Tt], op=ALU.mult),
                     reads=[("MU",)], writes=[("VAR",)])
                P.op('dve', lambda v: v.tensor_tensor(out=v3(VAR[:, 0:Tt]), in0=psv(s_e2, Th), in1=v3(VAR[:, 0:Tt]), op=ALU.subtract),
                     reads=[("ps", s_e2), ("VAR",)], writes=[("VAR",)])
                P.op('act', lambda a: a.activation(out=SQRT[:, 0:Tt], in_=VAR[:, 0:Tt], func=AF.Sqrt, bias=EPSC[:, 0:1]),
                     reads=[("VAR",), ("EPSC",)], writes=[("SQRT",)])
                act_preload(AF.Silu, 1)
                P.op('dve', lambda v: v.reciprocal(out=RSTD[:, 0:Tt], in_=SQRT[:, 0:Tt]),
                     reads=[("SQRT",)], writes=[("RSTD",)])
                for c0 in range(0, CA, 2):
                    tt_ = [nxt('tmp', NTMP), nxt('tmp', NTMP)]
                    for j in range(2):
                        c = c0 + j
                        P.op('dve', lambda v, c=c, t_=tt_[j]: v.tensor_tensor(
                            out=TMPb(t_, Tt), in0=ACONV(c, Tt), in1=MU[:, 0:Tt], op=ALU.subtract),
                            reads=[("R", 2 * c), ("R", 2 * c + 1), ("MU",)], writes=[("TMP", tt_[j])])
                    for j in range(2):
                        P.op('dve', lambda v, t_=tt_[j]: v.tensor_tensor(
                            out=TMPb(t_, Tt), in0=TMPb(t_, Tt), in1=RSTD[:, 0:Tt], op=ALU.mult),
                            reads=[("TMP", tt_[j]), ("RSTD",)], writes=[("TMP", tt_[j])])
                    for j in range(2):
                        c = c0 + j
                        P.op('act', lambda a, c=c, t_=tt_[j]: a.activation(
                            out=MAc(c, 0, Tt), in_=TMPb(t_, Tt), func=AF.Silu, scale=pb(40 + c), bias=pb(48 + c)),
                            reads=[("TMP", tt_[j]), ("PB",)], writes=[("R", c)])

                if last_tile:
                    for si, seg in enumerate(segs):
                        na_dst = nap if seg['kind'] == 'p' else nas
                        nb_dst = nbp if seg['kind'] == 'p' else nbs
                        s_ = nxt('ps', NPS)

                        def emit(t, s_=s_, si=si):
                            for j in range(2):
                                ins = t.transpose(PS[0:120, 2 * s_, j * 128:(j + 1) * 128],
                                                  STA[:, si, j].rearrange("p t q -> p (t q)"), IDENT[:, :])
                            return ins
                        P.op('pe', emit, reads=[("ST", si), ("IDENT",)], writes=[("ps", s_)])
                        act_copy(OUTA[:, si, :], PS[0:120, 2 * s_, 0:256], reads=[("ps", s_)], writes=[("OUTA", si)])
                        P.op('sp', lambda s, si=si, na_dst=na_dst: s.dma_start(
                            out=na_dst[l].rearrange("t (q n) -> (t q) n", n=256), in_=OUTA[:, si, :]),
                            reads=[("OUTA", si)], dma="OA%d" % si, is_output=True)
                        s_ = nxt('ps', NPS)
                        P.op('pe', lambda t, s_=s_, si=si: t.transpose(
                            PS[0:16, 2 * s_, 0:128], STB[:, si].rearrange("p t c -> p (t c)"), IDENT[:, :]),
                            reads=[("ST", si), ("IDENT",)], writes=[("ps", s_)])
                        act_copy(OUTB[:, si, :], PS[0:16, 2 * s_, 0:128], reads=[("ps", s_)], writes=[("OUTB", si)])
                        P.op('sp', lambda s, si=si, nb_dst=nb_dst: s.dma_start(
                            out=nb_dst[l].rearrange("t (c n) -> (t c) n", n=128), in_=OUTB[:, si, :]),
                            reads=[("OUTB", si)], dma="OB%d" % si, is_output=True)

                w_out_v = w_out[l].rearrange("(kc p) n -> p kc n", p=128)
                MK = [("R", c) for c in range(CA)] + [("MB", c) for c in range(CA)]

                def m_rhs(k, n):
                    if k < CA:
                        return MAc(k, n * Th, (n + 1) * Th)
                    return MBc(k - CA, n * Th, (n + 1) * Th)
                pair = []
                for j in (0, 1):
                    wi = wload(w_out_v[:, :, j * 128:(j + 1) * 128], 16)
                    pair.append((wi, nxt('ps', NPS)))
                for k0, k1, keys in ((CA, 2 * CA, [("MB", c) for c in range(CA)]), (0, CA, [("R", c) for c in range(CA)])):
                    for wi, s_ in pair:
                        def emit_h(t, wi=wi, s_=s_, k0=k0, k1=k1):
                            for k in range(k0, k1):
                                for n in range(2):
                                    ins = t.matmul(PS[:, 2 * s_ + n, 0:Th], lhsT=W[:, wi, k, :], rhs=m_rhs(k, n),
                                                   start=(k == CA), stop=(k == CA - 1))
                            return ins
                        P.op('pe', emit_h, reads=keys + [("W", wi)], writes=[("ps", s_)])
                st2 = rms_begin()
                pend = []
                for j in range(DC):
                    if j < 2:
                        s_ = pair[j][1]
                    else:
                        wi = wload(w_out_v[:, :, j * 128:(j + 1) * 128], 16)
                        s_ = mm_job(16, lambda k, wi=wi: W[:, wi, k, :], m_rhs, MK + [("W", wi)], Th)
                    P.op('dve', lambda v, s_=s_, j=j: v.tensor_tensor(
                        out=v3(X[:, j, 0:Tt]), in0=psv(s_, Th), in1=v3(X[:, j, 0:Tt]), op=ALU.add),
                        reads=[("ps", s_), ("X", j)], writes=[("X", j)])
                    pend.append(j)
                    if len(pend) > 2:
                        rms_chunk(st2, pend.pop(0), Tt, Th)
                    P.op('act', lambda a, j=j: a.activation(out=H[:, j, 0:Tt], in_=X[:, j, 0:Tt], func=AF.Copy, scale=pb(16 + j)),
                         reads=[("X", j), ("PB",)], writes=[("H", j)])
                for j in pend:
                    rms_chunk(st2, j, Tt, Th)

                rms_end(st2, Tt, Th, AF.Silu)
                w_up_v = w_up[l].rearrange("(kc p) n -> p kc n", p=128)

                def up_quarter(q):
                    bq = q % 2
                    for i in range(11):
                        f = q * 11 + i
                        ss = []
                        for gv in range(2):
                            col0 = gv * DFF + f * 128
                            wi = wload(w_up_v[:, :, col0:col0 + 128], 16)
                            if False and f == 0:
                                ss.append(mm_job(16, lambda k, wi=wi: W[:, wi, k, :],
                                                 lambda k, n: H[:, k, n * Th:(n + 1) * Th], [("W", wi)], Th,
                                                 fine=lambda k: ("H", k)))
                            else:
                                ss.append(mm_job(16, lambda k, wi=wi: W[:, wi, k, :],
                                                 lambda k, n: H[:, k, n * Th:(n + 1) * Th], HK + [("W", wi)], Th))
                        ug = nxt('ugv', 2)
                        U = UGVb(ug)
                        tg = [nxt('tmpf', NTMP), nxt('tmpf', NTMP)]
                        for gv in range(2):
                            P.op('dve', lambda v, gv=gv, U=U, s_=ss[gv]: v.tensor_tensor(
                                out=v3(U[:, gv, 2:2 + Tt]), in0=psv(s_, Th), in1=v3(RSTD[:, 0:Tt]), op=ALU.mult),
                                reads=[("ps", ss[gv]), ("RSTD",)], writes=[("UGV", ug, gv)])
                        for gv in range(2):
                            P.op('dve', lambda v, gv=gv, U=U, t_=tg[gv], f=f: v.tensor_scalar(
                                out=TMPFb(t_, Tt), in0=U[:, gv, 2:2 + Tt], scalar1=pb(328 + 2 * 88 + gv * CF + f), scalar2=0.0,
                                op0=ALU.mult, op1=ALU.add),
                                reads=[("UGV", ug, gv), ("PB",)], writes=[("TMPF", tg[gv])])
                        dk = [("UGV", ug, 0), ("UGV", ug, 1)]
                        for si, seg in enumerate(segs):
                            col, n = seg['col'], seg['n']
                            if seg['kind'] == 'p' and seg['first']:
                                srcap, rk = ZERO[:, 0:4].rearrange("p (g t) -> p g t", g=2), [("ZERO",)]
                            elif seg['kind'] == 'p':
                                srcap, rk = HISTF[:, l, :, f, :], [("HISTF",)]
                            else:
                                srcap, rk = SFT[:, l].rearrange("p t (g f) -> p g t f", g=2)[:, :, :, f], [("SFT", l)]
                            act_copy(U[:, :, col:col + 2], srcap, reads=rk + (dk if col > 0 else []), writes=[("UGVH", ug)])
                            tail = U[:, :, col + n:col + n + 2]
                            if seg['kind'] == 'p' and not seg['last']:
                                act_copy(HISTF[:, l, :, f, :], tail, reads=dk, writes=[("HISTF",)])
                            if seg['last']:
                                act_copy(STF[:, si].rearrange("p t (g f) -> p g t f", g=2)[:, :, :, f], tail,
                                         reads=dk, writes=[("STF", si)])
                        for k in (1, 0):
                            for gv in range(2):
                                P.op('dve', lambda v, gv=gv, U=U, t_=tg[gv], f=f, k=k: v.scalar_tensor_tensor(
                                    out=TMPFb(t_, Tt), in0=U[:, gv, k:k + Tt], scalar=pb(328 + k * 88 + gv * CF + f),
                                    in1=TMPFb(t_, Tt), op0=ALU.mult, op1=ALU.add),
                                    reads=[("UGV", ug, gv), ("UGVH", ug), ("TMPF", tg[gv]), ("PB",)], writes=[("TMPF", tg[gv])])
                        sl_ = nxt('tmpf', NTMP)
                        P.op('act', lambda a, sl_=sl_, t_=tg[0]: a.activation(out=TMPFb(sl_, Tt), in_=TMPFb(t_, Tt), func=AF.Silu),
                             reads=[("TMPF", tg[0])], writes=[("TMPF", sl_)])
                        P.op('dve', lambda v, sl_=sl_, t_=tg[1], bq=bq, i=i: v.tensor_tensor(
                            out=ACTQc(bq, i, 0, Tt), in0=TMPFb(sl_, Tt), in1=TMPFb(t_, Tt), op=ALU.mult),
                            reads=[("TMPF", sl_), ("TMPF", tg[1])], writes=[("ACTQ", bq, i)])

                def down_quarter(q, with_stats=False):
                    bq = q % 2
                    wd_v = w_down[l][q * 1408:(q + 1) * 1408, :].rearrange("(kc p) n -> p kc n", p=128)
                    AK = [("ACTQ", bq, i) for i in range(11)]
                    st3 = rms_begin() if with_stats else None
                    pend = []
                    for j in range(DC):
                        wi = wload(wd_v[:, :, j * 128:(j + 1) * 128], 11)
                        s_ = mm_job(11, lambda k, wi=wi: W[:, wi, k, :],
                                    lambda k, n: ACTQc(bq, k, n * Th, (n + 1) * Th), AK + [("W", wi)], Th)
                        P.op('dve', lambda v, s_=s_, j=j: v.tensor_tensor(
                            out=v3(X[:, j, 0:Tt]), in0=psv(s_, Th), in1=v3(X[:, j, 0:Tt]), op=ALU.add),
                            reads=[("ps", s_), ("X", j)], writes=[("X", j)])
                        if with_stats:
                            pend.append(j)
                            if len(pend) > 2:
                                rms_chunk(st3, pend.pop(0), Tt, Th)
                    if with_stats:
                        for j in pend:
                            rms_chunk(st3, j, Tt, Th)
                        rms_end(st3, Tt, Th, AF.Sigmoid if l < n_layers - 1 else None)
                        nready['v'] = True

                up_quarter(0)
                up_quarter(1)
                down_quarter(0)
                up_quarter(2)
                down_quarter(1)
                up_quarter(3)
                if last_tile:
                    for si, seg in enumerate(segs):
                        nf_dst = nfp if seg['kind'] == 'p' else nfs
                        s_ = nxt('ps', NPS)

                        def emit(t, s_=s_, si=si):
                            for tt in range(2):
                                ins = t.transpose(PS[0:88, 2 * s_, tt * 128:(tt + 1) * 128], STF[:, si, tt, :], IDENT[:, :])
                            return ins
                        P.op('pe', emit, reads=[("STF", si), ("IDENT",)], writes=[("ps", s_)])
                        act_copy(OUTF[:, si].rearrange("c t n -> c (t n)"), PS[0:88, 2 * s_, 0:256],
                                 reads=[("ps", s_)], writes=[("OUTF", si)])
                        P.op('sp', lambda s, si=si, nf_dst=nf_dst: s.dma_start(
                            out=nf_dst[l].rearrange("t (c n) -> c t n", n=128), in_=OUTF[:, si]),
                            reads=[("OUTF", si)], dma="OF%d" % si, is_output=True)
                down_quarter(2)
                down_quarter(3, with_stats=True)

            for l in range(n_layers):
                do_layer(l)

            if not nready['v']:
                rms_stats(Tt, Th)
            nready['v'] = False
            rms_apply(Tt, lambda c: PBF[:, c:c + 1], lambda c: X[:, c, 0:Tt], lambda c: ("X", c))
            for seg in segs:
                dst = yp if seg['kind'] == 'p' else ys
                a = 0
                while a < seg['n']:
                    nb = min(128, seg['n'] - a)
                    tok = seg['tok0'] + a
                    col = seg['col'] + a
                    yo = nxt('yout', 2)
                    for half in range(2):
                        s_ = nxt('ps', NPS)

                        def emit(t, s_=s_, nb=nb, half=half, col=col):
                            for j in range(8):
                                cc = half * 8 + j
                                ins = t.transpose(PS[0:nb, 2 * s_ + j // 4, (j % 4) * 128:(j % 4 + 1) * 128],
                                                  X[:, cc, col:col + nb], IDENT[:, :])
                            return ins
                        P.op('pe', emit, reads=XK + [("IDENT",)], writes=[("ps", s_)])
                        copy_alt(YOUTb(yo)[0:nb, half * 1024:(half + 1) * 1024].rearrange("p (a n) -> p a n", a=2),
                                 PS[0:nb, 2 * s_:2 * s_ + 2, :], reads=[("ps", s_)], writes=[("YOUT", yo, half)])
                    P.op('sp', lambda s, yo=yo, nb=nb, tok=tok, dst=dst:
                         s.dma_start(out=dst[tok:tok + nb, :], in_=YOUTb(yo)[0:nb, :]),
                         reads=[("YOUT", yo, 0), ("YOUT", yo, 1)], dma="YO%d" % yo, is_output=True)
                    a += nb

        for ti, tile in enumerate(TILES):
            if ti == len(TILES) - 1:
                load_states()
            do_tile(ti, tile)


    P0 = Prog()
    slabs = []
    emit_all(P0, slabs, True)
    P = Prog()
    emit_all(P, slabs, False)
    P.finalize()

    esem = {e: es.enter_context(nc.semaphore("sem_" + e)) for e in ENGS}
    dsem = {n: es.enter_context(nc.semaphore("dsem_" + n)) for n in sorted(P.dcnt)}
    with nc.Block() as block:
        @block.tensor
        def _(t):
            P.replay('pe', t, esem, dsem)

        @block.scalar
        def _(a):
            P.replay('act', a, esem, dsem)

        @block.vector
        def _(v):
            P.replay('dve', v, esem, dsem)

        @block.gpsimd
        def _(g):
            P.replay('pool', g, esem, dsem)

        @block.sync
        def _(s):
            P.replay('sp', s, esem, dsem)
    es.close()
    return nc


def make_in_maps(inp, n_cores=8):
    f = lambda a: np.ascontiguousarray(np.asarray(a, dtype=np.float32))
    shared = dict(
        n1g=f(inp["norm1_g"]), w_in=f(inp["w_in"]), caw=f(inp["conv_a_w"]), cab=f(inp["conv_a_b"]),
        lng=f(inp["ln_a_g"]), lnb=f(inp["ln_a_b"]), cbw=f(inp["conv_b_w"]), w_out=f(inp["w_out"]),
        n2g=f(inp["norm2_g"]), w_up=f(inp["w_up"]), cfw=f(inp["conv_ffn_w"]), w_down=f(inp["w_down"]),
        fg=f(inp["final_g"]))
    xp = f(inp["x_prompt"]); xs = f(inp["x_sample"])
    sa = f(inp["state_conv_a"]); sb = f(inp["state_conv_b"]); sf = f(inp["state_ffn"])
    maps = []
    for b in range(n_cores):
        m = dict(shared)
        m.update(xp=f(xp[b]), xs=f(xs[b]), sa=f(sa[:, b]), sb=f(sb[:, b]), sf=f(sf[:, b]))
        maps.append(m)
    return maps


def kernel(**inputs):
    nc = build_program()
    maps = make_in_maps(inputs)
    res = run_bass_kernel_spmd(nc, maps, core_ids=list(range(8)))
    R = res.results
    st = lambda name, ax: np.stack([np.asarray(R[b][name], dtype=np.float32) for b in range(8)], axis=ax)
    return (st("yp", 0), st("ys", 0), st("nap", 1), st("nbp", 1), st("nfp", 1),
            st("nas", 1), st("nbs", 1), st("nfs", 1))
```
